# Optimizing a Trainium2 kernel written in Bass

```python
import jax, jax.numpy as jnp
from jax import lax
import numpy as np

D_MODEL = 2048
BATCH = 32
SEQ = 256
DEPTH = 4
DEC_BATCH = 4
DEC_SEQ = 1024
PAST_LEN = 256

GRID_W = 64
HEAD_DIM = 128
ATTN_WIDTH = D_MODEL // 2
ATTN_HEADS = ATTN_WIDTH // HEAD_DIM
ATTN_KV_HEADS = ATTN_HEADS // 4
GQA_GROUP = ATTN_HEADS // ATTN_KV_HEADS
KV_WIDTH = ATTN_KV_HEADS * HEAD_DIM
GLA_VAL_WIDTH = D_MODEL - ATTN_WIDTH
GLA_KEY_WIDTH = GLA_VAL_WIDTH // 2
GLA_HEADS = 8
GLA_DK = GLA_KEY_WIDTH // GLA_HEADS
GLA_DV = GLA_VAL_WIDTH // GLA_HEADS
GLA_RANK = 16
GLA_TAU = 16.0
GLA_CHUNK = 64
D_FF = 4 * D_MODEL
Q_BLOCK = 128
ROPE_THETA = 10000.0
ROPE_AXIS_DIM = HEAD_DIM // 2
ROPE_FREQS = ROPE_AXIS_DIM // 2
N_MOD = 6
PROJ_WIDTH = ATTN_WIDTH + 2 * KV_WIDTH + 2 * GLA_KEY_WIDTH + 2 * GLA_VAL_WIDTH + 2 * GLA_RANK
EPS = 1e-6

kernel_name = 'hybrid_gqa_gla_prefix_diffusion_step'


def rmsnorm(x, g):
    xf = x.astype(jnp.float32)
    y = xf * lax.rsqrt(jnp.mean(jnp.square(xf), axis=-1, keepdims=True) + EPS)
    return (y * g.astype(jnp.float32)).astype(x.dtype)


def axial_rope_tables(n_tokens):
    rows = n_tokens // GRID_W
    r = jnp.repeat(jnp.arange(rows, dtype=jnp.float32), GRID_W)
    col = jnp.tile(jnp.arange(GRID_W, dtype=jnp.float32), rows)
    inv = ROPE_THETA ** (-jnp.arange(ROPE_FREQS, dtype=jnp.float32) / ROPE_FREQS)
    ang_r = r[:, None] * inv[None, :]
    ang_c = col[:, None] * inv[None, :]
    return (jnp.cos(ang_r), jnp.sin(ang_r), jnp.cos(ang_c), jnp.sin(ang_c))


def _rotate(x, cos, sin):
    x1, x2 = x[..., :ROPE_FREQS], x[..., ROPE_FREQS:]
    cos, sin = cos[None, :, None, :], sin[None, :, None, :]
    return jnp.concatenate([x1 * cos - x2 * sin, x2 * cos + x1 * sin], axis=-1)


def apply_axial_rope(x, rope):
    cos_r, sin_r, cos_c, sin_c = rope
    xf = x.astype(jnp.float32)
    out = jnp.concatenate([_rotate(xf[..., :ROPE_AXIS_DIM], cos_r, sin_r),
                           _rotate(xf[..., ROPE_AXIS_DIM:], cos_c, sin_c)], axis=-1)
    return out.astype(x.dtype)


def block_attention(q, k, v):
    B, T, H, Dh = q.shape
    nb = T // Q_BLOCK
    qb = q.reshape(B, nb, Q_BLOCK, ATTN_KV_HEADS, GQA_GROUP, Dh).transpose(1, 0, 2, 3, 4, 5)
    kf = k.astype(jnp.float32)
    vf = v.astype(jnp.float32)
    scale = Dh ** -0.5

    def one_block(qblk):
        s = jnp.einsum('bqkgd,bskd->bkgqs', qblk.astype(jnp.float32), kf) * scale
        p = jax.nn.softmax(s, axis=-1)
        o = jnp.einsum('bkgqs,bskd->bqkgd', p, vf)
        return o.reshape(B, Q_BLOCK, H * Dh).astype(q.dtype)

    out = lax.map(one_block, qb)
    return out.transpose(1, 0, 2, 3).reshape(B, T, H * Dh)


def _gla_chunk_step(state, blk):
    q, k, v, g = blk
    b = jnp.cumsum(g, axis=2)
    o_inter = jnp.einsum('bhcd,bhde->bhce', q * jnp.exp(b), state)
    C = q.shape[2]
    lower = jnp.tril(jnp.ones((C, C), dtype=bool))[:, :, None]
    diff = b[:, :, :, None, :] - b[:, :, None, :, :]
    decay = jnp.exp(jnp.where(lower, diff, -jnp.inf))
    scores = jnp.sum(q[:, :, :, None, :] * k[:, :, None, :, :] * decay, axis=-1)
    o_intra = jnp.einsum('bhij,bhje->bhie', scores, v)
    b_last = b[:, :, -1, :]
    k_dec = k * jnp.exp(b_last[:, :, None, :] - b)
    new_state = jnp.exp(b_last)[..., None] * state + jnp.einsum('bhcd,bhce->bhde', k_dec, v)
    return new_state, o_inter + o_intra


def gla_scan(q, k, v, g, s0):
    B, T, H, _ = q.shape
    nc = T // GLA_CHUNK

    def to_chunks(a):
        return a.astype(jnp.float32).reshape(B, nc, GLA_CHUNK, H, a.shape[-1]).transpose(1, 0, 3, 2, 4)

    s_final, o = lax.scan(_gla_chunk_step, s0.astype(jnp.float32),
                          (to_chunks(q), to_chunks(k), to_chunks(v), to_chunks(g)))
    o = o.transpose(1, 0, 3, 2, 4).reshape(B, T, H, v.shape[-1])
    return o, s_final


def gla_bidirectional(q, k, v, g_fwd, g_bwd, s_f0, s_b0):
    o_f, s_f = gla_scan(q, k, v, g_fwd, s_f0)
    rev = lambda a: jnp.flip(a, axis=1)
    o_b, s_b = gla_scan(rev(q), rev(k), rev(v), rev(g_bwd), s_b0)
    return o_f + rev(o_b), s_f, s_b


def trunk_layer(x, mod, rope, ctx, norm1, w_in, q_norm, k_norm, w_gate_fwd, b_gate_fwd,
                w_gate_bwd, b_gate_bwd, gla_norm, w_out, norm2, w_up, w_down):
    B, T, _ = x.shape
    shift1, scale1, gate1, shift2, scale2, gate2 = jnp.split(mod, N_MOD, axis=-1)
    h = rmsnorm(x, norm1) * (1.0 + scale1) + shift1
    proj = h @ w_in
    sizes = [ATTN_WIDTH, KV_WIDTH, KV_WIDTH, GLA_KEY_WIDTH, GLA_KEY_WIDTH,
             GLA_VAL_WIDTH, GLA_VAL_WIDTH, GLA_RANK, GLA_RANK]
    offsets = [int(s) for s in np.cumsum(sizes)[:-1]]
    q_a, k_a, v_a, q_g, k_g, v_g, o_gate, r_f, r_b = jnp.split(proj, offsets, axis=-1)

    q_a = rmsnorm(q_a.reshape(B, T, ATTN_HEADS, HEAD_DIM), q_norm)
    k_a = rmsnorm(k_a.reshape(B, T, ATTN_KV_HEADS, HEAD_DIM), k_norm)
    v_a = v_a.reshape(B, T, ATTN_KV_HEADS, HEAD_DIM)

    g_f = jax.nn.log_sigmoid((r_f @ w_gate_fwd + b_gate_fwd).astype(jnp.float32)) / GLA_TAU
    g_b = jax.nn.log_sigmoid((r_b @ w_gate_bwd + b_gate_bwd).astype(jnp.float32)) / GLA_TAU
    q_g = q_g.reshape(B, T, GLA_HEADS, GLA_DK) * (GLA_DK ** -0.5)
    k_g = k_g.reshape(B, T, GLA_HEADS, GLA_DK)
    v_g = v_g.reshape(B, T, GLA_HEADS, GLA_DV)
    g_f = g_f.reshape(B, T, GLA_HEADS, GLA_DK)
    g_b = g_b.reshape(B, T, GLA_HEADS, GLA_DK)

    if ctx is None:
        keys, vals = k_a, v_a
        s_f0 = jnp.zeros((B, GLA_HEADS, GLA_DK, GLA_DV), jnp.float32)
        s_b0 = jnp.zeros((B, GLA_HEADS, GLA_DK, GLA_DV), jnp.float32)
    else:
        k_ctx, v_ctx, s_f0, s_b0 = ctx
        q_a = apply_axial_rope(q_a, rope)
        k_lat = apply_axial_rope(k_a, rope)
        keys = jnp.concatenate([k_ctx.astype(k_lat.dtype), k_lat], axis=1)
        vals = jnp.concatenate([v_ctx.astype(v_a.dtype), v_a], axis=1)

    attn_out = block_attention(q_a, keys, vals)
    o_gla, s_f, s_b = gla_bidirectional(q_g, k_g, v_g, g_f, g_b, s_f0, s_b0)
    o_gla = (rmsnorm(o_gla, gla_norm).reshape(B, T, GLA_VAL_WIDTH) * jax.nn.silu(o_gate)).astype(x.dtype)

    mix = jnp.concatenate([attn_out, o_gla], axis=-1) @ w_out
    x = x + gate1 * mix
    h2 = rmsnorm(x, norm2) * (1.0 + scale2) + shift2
    x = x + gate2 * (jnp.square(jax.nn.relu(h2 @ w_up)) @ w_down)
    return x, (k_a, v_a, s_f, s_b)


def setup_inputs(seed: int = 0) -> dict:
    key = jax.random.key(seed)
    ks = jax.random.split(key, 24)
    f32 = jnp.float32
    n = lambda i, shape, s: jax.random.normal(ks[i], shape, f32) * s
    return {
        'x_prompt': n(0, (BATCH, SEQ, D_MODEL), 1.0),
        'x_sample': n(1, (DEC_BATCH, DEC_SEQ, D_MODEL), 1.0),
        'cache_k': n(2, (DEC_BATCH, DEPTH, PAST_LEN, ATTN_KV_HEADS, HEAD_DIM), 1.0),
        'cache_v': n(3, (DEC_BATCH, DEPTH, PAST_LEN, ATTN_KV_HEADS, HEAD_DIM), 1.0),
        'state_gla_fwd': n(4, (DEC_BATCH, DEPTH, GLA_HEADS, GLA_DK, GLA_DV), 0.5),
        'state_gla_bwd': n(5, (DEC_BATCH, DEPTH, GLA_HEADS, GLA_DK, GLA_DV), 0.5),
        'c': n(6, (DEC_BATCH, D_MODEL), 1.0),
        'c_ctx': n(7, (D_MODEL,), 1.0),
        'w_mod': n(8, (DEPTH, D_MODEL, N_MOD * D_MODEL), D_MODEL ** -0.5),
        'b_mod': n(9, (DEPTH, N_MOD * D_MODEL), 0.02),
        'norm1': 1.0 + n(10, (DEPTH, D_MODEL), 0.02),
        'w_in': n(11, (DEPTH, D_MODEL, PROJ_WIDTH), D_MODEL ** -0.5),
        'q_norm': 1.0 + n(12, (DEPTH, HEAD_DIM), 0.02),
        'k_norm': 1.0 + n(13, (DEPTH, HEAD_DIM), 0.02),
        'w_gate_fwd': n(14, (DEPTH, GLA_RANK, GLA_KEY_WIDTH), GLA_RANK ** -0.5),
        'b_gate_fwd': n(15, (DEPTH, GLA_KEY_WIDTH), 0.1),
        'w_gate_bwd': n(16, (DEPTH, GLA_RANK, GLA_KEY_WIDTH), GLA_RANK ** -0.5),
        'b_gate_bwd': n(17, (DEPTH, GLA_KEY_WIDTH), 0.1),
        'gla_norm': 1.0 + n(18, (DEPTH, GLA_DV), 0.02),
        'w_out': n(19, (DEPTH, D_MODEL, D_MODEL), D_MODEL ** -0.5),
        'norm2': 1.0 + n(20, (DEPTH, D_MODEL), 0.02),
        'w_up': n(21, (DEPTH, D_MODEL, D_FF), D_MODEL ** -0.5),
        'w_down': n(22, (DEPTH, D_FF, D_MODEL), D_FF ** -0.5),
    }


def reference(x_prompt, x_sample, cache_k, cache_v, state_gla_fwd, state_gla_bwd, c, c_ctx,
              w_mod, b_mod, norm1, w_in, q_norm, k_norm, w_gate_fwd, b_gate_fwd,
              w_gate_bwd, b_gate_bwd, gla_norm, w_out, norm2, w_up, w_down):
    rope = axial_rope_tables(x_sample.shape[1])
    silu_ctx = jax.nn.silu(c_ctx)
    silu_c = jax.nn.silu(c)
    y_p, y_s = x_prompt, x_sample
    ks, vs, sfs, sbs = [], [], [], []
    for l in range(DEPTH):
        lw = (norm1[l], w_in[l], q_norm[l], k_norm[l], w_gate_fwd[l], b_gate_fwd[l],
              w_gate_bwd[l], b_gate_bwd[l], gla_norm[l], w_out[l], norm2[l], w_up[l], w_down[l])
        mod_ctx = silu_ctx @ w_mod[l] + b_mod[l]
        y_p, (k_l, v_l, sf_l, sb_l) = trunk_layer(y_p, mod_ctx, None, None, *lw)
        ks.append(k_l)
        vs.append(v_l)
        sfs.append(sf_l)
        sbs.append(sb_l)
        mod_lat = (silu_c @ w_mod[l] + b_mod[l])[:, None, :]
        ctx = (cache_k[:, l], cache_v[:, l], state_gla_fwd[:, l], state_gla_bwd[:, l])
        y_s, _ = trunk_layer(y_s, mod_lat, rope, ctx, *lw)
    new_cache_k = jnp.stack(ks, axis=1)
    new_cache_v = jnp.stack(vs, axis=1)
    new_state_gla_fwd = jnp.stack(sfs, axis=1)
    new_state_gla_bwd = jnp.stack(sbs, axis=1)
    return (y_p, y_s, new_cache_k, new_cache_v, new_state_gla_fwd, new_state_gla_bwd)
```

```python
import contextlib
import numpy as np
import concourse.bass as bass
import concourse.mybir as mybir
from concourse.bass_utils import run_bass_kernel_spmd

F32 = mybir.dt.float32
BF16 = mybir.dt.bfloat16
AF = mybir.ActivationFunctionType
ALU = mybir.AluOpType
AX = mybir.AxisListType

D = 2048
KC = 16
T = 1024
NT = 8
EPS = 1e-6
PROJ = 4640
DFF = 8192


class Trk:
    def __init__(self, nc, prog):
        self.nc = nc
        self.eng = {"pe": nc.tensor, "act": nc.scalar, "dve": nc.vector, "pool": nc.gpsimd, "sp": nc.sync}
        self.prog = prog
        self.cnt = {e: 0 for e in self.eng}
        self.dcnt = {}
        self.waited = {e: {} for e in self.eng}
        self.lastw = {}
        self.readers = {}
        self.final = []
        self.dead = False

    def begin(self, e, reads, writes, accum=()):
        deps = {}

        def add(tok):
            sem, val = tok
            if deps.get(sem.num, (None, 0))[1] < val:
                deps[sem.num] = (sem, val)

        own = self.prog[e].num if e in self.prog else -1
        for k in reads:
            if k in self.lastw:
                add(self.lastw[k])
        for k in writes:
            if k in self.lastw:
                tok = self.lastw[k]
                if not (k in accum and tok[0].num == own):
                    add(tok)
            for tok in self.readers.get(k, {}).values():
                add(tok)
        eng = self.eng[e]
        for num, (sem, val) in deps.items():
            if self.waited[e].get(num, 0) < val:
                eng.wait_ge(sem, val)
                self.waited[e][num] = val

    def end(self, e, inst, reads, writes, dsem=None):
        if dsem is None:
            self.cnt[e] += 1
            inst.then_inc(self.prog[e], 1)
            tok = (self.prog[e], self.cnt[e])
        else:
            self.dcnt[dsem.num] = self.dcnt.get(dsem.num, 0) + 16
            inst.then_inc(dsem, 16)
            tok = (dsem, self.dcnt[dsem.num])
        for k in reads:
            self.readers.setdefault(k, {})[tok[0].num] = tok
        for k in writes:
            self.lastw[k] = tok
            self.readers[k] = {}
        return tok

    def op(self, e, reads, writes, fn, accum=()):
        if self.dead:
            return None
        self.begin(e, reads, writes, accum)
        inst = fn(self.eng[e])
        return self.end(e, inst, reads, writes)

    def group(self, e, reads, writes, fns, accum=()):
        if self.dead:
            return None
        self.begin(e, reads, writes, accum)
        inst = None
        for fn in fns:
            inst = fn(self.eng[e])
        return self.end(e, inst, reads, writes)

    def dma(self, e, reads, writes, dsem, fns):
        if self.dead:
            return None
        self.begin(e, reads, writes)
        tok = None
        for fn in fns:
            inst = fn(self.eng[e])
            tok = self.end(e, inst, reads, writes, dsem=dsem)
        return tok


class _Stop(Exception):
    pass


def build(DEPTH, stop_at=None):
    nc = bass.Bass("TRN2", target_bir_lowering=False)

    def chk(n):
        if stop_at == n:
            trk_box[0].dead = True

    trk_box = []

    def din(name, shape):
        return nc.dram_tensor(name, list(shape), F32, kind="ExternalInput").ap()

    def dout(name, shape):
        return nc.dram_tensor(name, list(shape), F32, kind="ExternalOutput").ap()

    xs = din("xs", [T, D]); xp = din("xp", [T, D])
    ck = din("ck", [DEPTH * 256, 256]); cv = din("cv", [DEPTH * 256, 256])
    sfd = din("sf", [DEPTH * 512, 128]); sbd = din("sb", [DEPTH * 512, 128])
    cc = din("cc", [32, 128])
    w_mod = din("w_mod", [DEPTH * D, 6 * D]); b_mod = din("b_mod", [DEPTH * 96, 128])
    norm1 = din("norm1", [DEPTH * 16, 128]); norm2 = din("norm2", [DEPTH * 16, 128])
    w_in = din("w_in", [DEPTH * D, PROJ])
    q_norm = din("q_norm", [DEPTH, 128]); k_norm = din("k_norm", [DEPTH, 128])
    wg = [din("wgf", [DEPTH * 16, 512]), din("wgb", [DEPTH * 16, 512])]
    bg = [din("bgf", [DEPTH, 512]), din("bgb", [DEPTH, 512])]
    gla_norm = din("gla_norm", [DEPTH, 128])
    w_out = din("w_out", [DEPTH * D, D])
    w_up = din("w_up", [DEPTH * D, DFF]); w_down = din("w_down", [DEPTH * DFF, D])
    cst = din("cst", [128, 7 * 128]); rope = din("rope", [T, 128])
    y_d = [dout("ys", [T, D]), dout("yp", [T, D])]
    nk = dout("nk", [4 * DEPTH * 256, 256]); nv = dout("nv", [4 * DEPTH * 256, 256])
    nst = [dout("nsf", [4 * DEPTH * 512, 128]), dout("nsb", [4 * DEPTH * 512, 128])]
    x_d = [xs, xp]
    st_d = [sfd, sbd]

    es = contextlib.ExitStack()
    with es:
        def sb_(name, shape, dt):
            return es.enter_context(nc.sbuf_tensor(name, list(shape), dt))

        def sem_(name):
            return es.enter_context(nc.semaphore(name))

        xT = sb_("xT", [128, KC, T], F32)
        hT = sb_("hT", [128, KC, T], BF16)
        mixT = sb_("mixT", [128, 8, T], BF16)
        slab = sb_("slab", [128, 2, 4096], BF16)
        scr = sb_("scr", [128, 7680], F32)
        cstt = sb_("cstt", [128, 7 * 128], F32)
        identb = sb_("identb", [128, 128], BF16)
        onesb = sb_("onesb", [128, 128], BF16)
        ropet = sb_("ropet", [128, NT, 128], F32)
        vecs = sb_("vecs", [128, DEPTH * 2 * 6 * 16], F32)
        n1T = sb_("n1T", [128, DEPTH * 16], F32)
        n2T = sb_("n2T", [128, DEPTH * 16], F32)
        gnT = sb_("gnT", [128, DEPTH], F32)
        scT = sb_("scT", [128, 32], BF16)
        gqk = sb_("gqk", [128, 2, 128], F32)
        wga = sb_("wga", [32, 2, 512], BF16)
        rTa = sb_("rTa", [32, 2, T], BF16)
        tmp = sb_("tmp", [128, 3, 512], F32)
        sqr = sb_("sqr", [128, 2, 512], BF16)
        pbuf = sqr
        kvst = sb_("kvst", [128, 2, 256], F32)
        sstg = sb_("sstg", [128, 2, 128], F32)
        rstd = sb_("rstd", [128, 512], F32)
        small = sb_("small", [128, 64], F32)
        ps = [es.enter_context(nc.psum_tensor("ps%d" % i, [128, 512], F32)) for i in range(6)]
        psbs = [es.enter_context(nc.psum_tensor("psb%d" % i, [128, 1024], BF16)) for i in range(2)]
        gtmp = sb_("gtmp", [128, 4, 128], F32)
        amb4 = sb_("amb4", [128, 4, 128], BF16)

        prog = {e: sem_("p_" + e) for e in ("pe", "act", "dve", "pool", "sp")}
        sl_sem = [sem_("slab0"), sem_("slab1")]
        misc_sem = sem_("misc")
        cst_sem = sem_("cst")
        rope_sem = sem_("rope")
        xin_sem = [sem_("xin0"), sem_("xin1")]
        ld_sem = {n: sem_("ld_" + n) for n in ("gqk", "wg")}
        ldkv_sem = [sem_("ldkv0"), sem_("ldkv1")]
        ldst_sem = [sem_("ldst0"), sem_("ldst1")]
        out_sem = {n: sem_("o_" + n) for n in ("y0", "y1", "k0", "k1", "v0", "v1", "s0", "s1")}

        trk = Trk(nc, prog)
        trk_box.append(trk)
        ident = cstt[:, 0:128]
        U_inc = [cstt[:, 128:256], cstt[:, 384:512]]
        U_suf = [cstt[:, 256:384], cstt[:, 512:640]]
        maskd = [cstt[:, 640:768], cstt[:, 768:896]]

        bmT = sb_("bmT", [128, DEPTH * 96], F32)[:, :]
        sbf = scr[:, :].bitcast(BF16)
        qT = sbf[:, 0:8192].rearrange("p (h t) -> p h t", h=8)
        kT = sbf[:, 8192:10752].rearrange("p (h t) -> p h t", h=2)
        vtok = sbf[:, 10752:13312].rearrange("p (k c) -> p k c", k=10)
        qtok = sbf[:, 13312:13824]
        qg = sbf[:, 0:1024]
        kgT = sbf[:, 1024:2048]
        vgtok = sbf[:, 2048:4096].rearrange("p (k c) -> p k c", k=NT)
        qe = sbf[:, 4096:6144].rearrange("p (d t) -> p d t", d=2)
        ke = sbf[:, 6144:8192].rearrange("p (d t) -> p d t", d=2)
        kd = sbf[:, 8192:10240].rearrange("p (d k c) -> p d k c", d=2, k=NT)
        og = sbf[:, 10240:12288].rearrange("p (h t) -> p h t", h=2)
        Sbf = sbf[:, 12288:12544].rearrange("p (d c) -> p d c", d=2)
        amb = sbf[:, 12544:12800].rearrange("p (r c) -> p r c", r=2)
        kgtok = scr[:, 6400:7424].rearrange("p (k c) -> p k c", k=NT)
        Sst = scr[:, 7424:7680].rearrange("p (d c) -> p d c", d=2)
        oacc = sb_("oacc", [128, 2, T], F32)
        elast = small[:, 0:16].rearrange("p (d k) -> p d k", d=2)
        gpb = sb_("gpb", [128, 2, 128], F32)
        etb = sb_("etb", [128, 2, 128], F32)

        pctr = [0]

        def nbank():
            i = pctr[0] % 6
            pctr[0] += 1
            return i

        rctr = {}

        def tk(r):
            return [("tmp", r, 0), ("tmp", r, 1)]

        def interleave(*gens):
            gens = list(gens)
            while gens:
                for g in list(gens):
                    try:
                        next(g)
                    except StopIteration:
                        gens.remove(g)

        def ring(name, n):
            i = rctr.get(name, 0)
            rctr[name] = i + 1
            return i % n

        plan = []
        issued = [0]
        taken = [0]
        slot_of = {}
        free_slots = [0, 1]
        cur_reg = [None]
        grp = [0]
        modcnt = {}

        def slab_view(i):
            spec = plan[i]
            slot = slot_of[i]
            view = slab[:, slot, 0:spec["kcs"] * spec["ncols"]].rearrange("p (k n) -> p k n", k=spec["kcs"])
            return view, ("slab", slot)

        def issue_ready():
            while issued[0] < len(plan) and free_slots:
                i = issued[0]
                slot = free_slots.pop(0)
                slot_of[i] = slot
                spec = plan[i]
                view, _ = slab_view(i)
                fns = []
                for (src, c0, n) in spec["parts"]:
                    srcv = src.rearrange("(k p) n -> p k n", p=128)
                    fns.append(lambda eng, srcv=srcv, c0=c0, n=n, view=view: eng.dma_start(
                        out=view[:, :, c0:c0 + n], in_=srcv, max_dma_last_dim=2048))
                trk.dma("pool", [], [("slab", slot)], sl_sem[slot], fns)
                issued[0] += 1

        def release(i):
            free_slots.append(slot_of[i])
            issue_ready()

        def take_mod():
            i = taken[0]
            tag = plan[i]["tag"]
            assert tag[0] == "mod"
            issue_ready()
            assert i in slot_of
            taken[0] += 1
            sv, skey = slab_view(i)
            consume_mod(tag[1], tag[2], sv, skey)
            release(i)

        def take_slab(tag):
            while plan[taken[0]]["tag"][0] == "mod":
                take_mod()
            i = taken[0]
            assert plan[i]["tag"] == tag, (plan[i]["tag"], tag)
            if cur_reg[0] is not None:
                release(cur_reg[0])
            issue_ready()
            assert i in slot_of
            taken[0] += 1
            cur_reg[0] = i
            grp[0] = 0
            return slab_view(i)

        def maybe_mod():
            grp[0] += 1
            if grp[0] == 2 and taken[0] < len(plan) and plan[taken[0]]["tag"][0] == "mod":
                take_mod()

        def add_plan(tag, kcs, ncols, parts):
            plan.append(dict(tag=tag, kcs=kcs, ncols=ncols, parts=parts))

        def mod_spec(l, s):
            return dict(tag=("mod", l, s), kcs=16, ncols=256,
                        parts=[(w_mod[l * D:(l + 1) * D, s * 256:(s + 1) * 256], 0, 256)])

        for s in range(48):
            plan.append(mod_spec(0, s))
        n_pre = len(plan)
        later_mods = [mod_spec(l, s) for l in range(1, DEPTH) for s in range(48)]
        for ph in _PHASES:
            for l in range(DEPTH):
                wl = w_in[l * D:(l + 1) * D, :]
                for s in range(4):
                    add_plan(("qa", ph, l, s), 16, 256, [(wl[:, s * 256:(s + 1) * 256], 0, 256)])
                add_plan(("ka", ph, l), 16, 256, [(wl[:, 1024:1280], 0, 256)])
                add_plan(("va", ph, l), 16, 256, [(wl[:, 1280:1536], 0, 256)])
                for s in range(4):
                    add_plan(("wo", ph, l, 0, s), 8, 512, [(w_out[l * D:l * D + 1024, s * 512:(s + 1) * 512], 0, 512)])
                add_plan(("r", ph, l), 16, 32, [(wl[:, 4608:4640], 0, 32)])
                for hp in range(4):
                    add_plan(("gqk", ph, l, hp), 16, 256, [(wl[:, 1536 + hp * 128:1536 + (hp + 1) * 128], 0, 128),
                                                            (wl[:, 2048 + hp * 128:2048 + (hp + 1) * 128], 128, 128)])
                    add_plan(("gv", ph, l, hp), 16, 256, [(wl[:, 2560 + hp * 256:2560 + (hp + 1) * 256], 0, 256)])
                    add_plan(("go", ph, l, hp), 16, 256, [(wl[:, 3584 + hp * 256:3584 + (hp + 1) * 256], 0, 256)])
                for s in range(4):
                    add_plan(("wo", ph, l, 1, s), 8, 512, [(w_out[l * D + 1024:(l + 1) * D, s * 512:(s + 1) * 512], 0, 512)])
                for part in range(8):
                    for s in range(4):
                        c0 = part * 1024 + s * 256
                        add_plan(("up", ph, l, part, s), 16, 256, [(w_up[l * D:(l + 1) * D, c0:c0 + 256], 0, 256)])
                    for s in range(4):
                        r0 = l * DFF + part * 1024
                        add_plan(("dn", ph, l, part, s), 8, 512, [(w_down[r0:r0 + 1024, s * 512:(s + 1) * 512], 0, 512)])

        if later_mods:
            regs = plan[n_pre:]
            newp = plan[:n_pre]
            for r_ in regs:
                newp.append(r_)
                if later_mods:
                    newp.append(later_mods.pop(0))
            assert not later_mods
            plan[:] = newp

        def loadT(dst, src, n, eng="dve"):
            st = tmp[0:n, 0, 0:128]
            trk.dma("sp", [], tk(0), misc_sem, [lambda e: e.dma_start(out=st, in_=src)])
            b = 6 if False else nbank()
            trk.op("pe", tk(0) + ["cst"], [("ps", b)],
                   lambda e: e.transpose(out=ps[b][:, 0:n], in_=st, identity=ident[0:n, 0:n]))
            trk.op(eng, [("ps", b)], [dst_key(dst)],
                   lambda e: e.tensor_copy(out=dst, in_=ps[b][:, 0:n]))

        def dst_key(ap):
            return "smallvecs"

        def barrier():
            if trk.dead:
                return
            for e in ("pe", "act", "dve"):
                for o in ("pe", "act", "dve"):
                    if o != e and trk.waited[e].get(prog[o].num, 0) < trk.cnt[o]:
                        trk.eng[e].wait_ge(prog[o], trk.cnt[o])
                        trk.waited[e][prog[o].num] = trk.cnt[o]

        def cp(en, out, in_):
            if en == "act":
                return lambda e: e.activation(out=out, in_=in_, func=AF.Copy)
            return lambda e: e.tensor_copy(out=out, in_=in_)

        trk.dma("sp", [], ["cst"], cst_sem, [lambda e: e.dma_start(out=cstt[:, :], in_=cst)])
        trk.dma("sp", [], ["rope"], rope_sem,
                [lambda e: e.dma_start(out=ropet[:, :, :], in_=rope.rearrange("(k p) c -> p k c", p=128))])
        trk.op("dve", ["cst"], ["identb"], lambda e: e.tensor_copy(out=identb[:, :], in_=ident))
        trk.op("dve", [], ["onesb"], lambda e: e.memset(onesb[:, :], 1.0))
        trk.op("dve", [], ["rTa"], lambda e: e.memset(rTa[:, :, :], 1.0))
        loadT(n1T[:, :], norm1, DEPTH * 16)
        loadT(n2T[:, :], norm2, DEPTH * 16)
        loadT(gnT[:, :], gla_norm, DEPTH)
        for l in range(DEPTH):
            loadT(bmT[:, l * 96:(l + 1) * 96], b_mod[l * 96:(l + 1) * 96, :], 96)
        loadT(small[:, 32:64], cc, 32)
        trk.op("act", ["smallvecs"], ["scT"],
               lambda e: e.activation(out=scT[:, :], in_=small[:, 32:64], func=AF.Silu))

        bmv = bmT.rearrange("p (l c) -> p l c", l=DEPTH)
        scv = scT[:, :].rearrange("p (r k) -> p k r", r=2)
        vv = vecs[:, :].rearrange("p (l r j k) -> p l r j k", l=DEPTH, r=2, j=6)

        def consume_mod(l, s, sv, skey):
            b = nbank()
            fns = []
            for ct in range(2):
                for kc in range(KC):
                    fns.append(lambda e, ct=ct, kc=kc: e.matmul(
                        ps[b][:, ct * 2:ct * 2 + 2], lhsT=sv[:, kc, ct * 128:(ct + 1) * 128], rhs=scv[:, kc, :],
                        start=(kc == 0), stop=(kc == KC - 1)))
            trk.group("pe", [skey, "scT"], [("ps", b)], fns)
            j, k0 = s // 8, 2 * (s % 8)
            trk.op("dve", [("ps", b), "smallvecs"], [("vecs", l)],
                   lambda e: e.tensor_tensor(
                       out=vv[:, l, :, j, k0:k0 + 2],
                       in0=ps[b][:, 0:4].rearrange("p (c r) -> p r c", r=2),
                       in1=bmv[:, l, s * 2:s * 2 + 2].unsqueeze(1).to_broadcast([128, 2, 2]), op=ALU.add))
            modcnt[l] = modcnt.get(l, 0) + 1

        def mod_fixups(l):
            if trk.dead:
                return
            assert modcnt.get(l, 0) == 48, (l, modcnt)
            for r in range(2):
                for j, nrm in ((1, n1T), (4, n2T)):
                    trk.op("dve", [("vecs", l), "smallvecs"], [("vecs", l)],
                           lambda e, r=r, j=j, nrm=nrm: e.scalar_tensor_tensor(
                               out=vv[:, l, r, j, :], in0=vv[:, l, r, j, :], scalar=1.0, in1=nrm[:, l * 16:(l + 1) * 16],
                               op0=ALU.add, op1=ALU.mult))

        for s in range(48):
            take_mod()
        mod_fixups(0)

        def rstd_from_ps(b, n, inv_n):
            trk.op("act", [("ps", b)], ["rstd"], lambda e: e.activation(
                out=rstd[:, 0:n], in_=ps[b][:, 0:n], func=AF.Ln, scale=inv_n, bias=EPS))
            trk.op("act", ["rstd"], ["rstd"], lambda e: e.activation(out=rstd[:, 0:n], in_=rstd[:, 0:n], func=AF.Exp, scale=-0.5))

        def norm_mod(l, row, jg, jsh):
            for tg in range(2):
                tsl = slice(tg * 512, (tg + 1) * 512)
                b = nbank()
                for kc in range(KC):
                    r_ = ring("sqr", 2)
                    trk.op("act", [("x", kc, tg)], [("sqr", r_)], lambda e, kc=kc, r_=r_: e.activation(
                        out=sqr[:, r_, :], in_=xT[:, kc, tsl], func=AF.Square))
                    trk.op("pe", [("sqr", r_), "onesb"], [("ps", b)], lambda e, kc=kc, r_=r_: e.matmul(
                        ps[b][:, :], lhsT=onesb[:, :], rhs=sqr[:, r_, :], start=(kc == 0), stop=(kc == KC - 1)),
                        accum=[("ps", b)])
                rstd_from_ps(b, 512, 1.0 / D)
                for kc in range(KC):
                    r_ = ring("tmp", 3)
                    trk.op("dve", [("x", kc, tg), "rstd", ("vecs", l)], tk(r_), lambda e, kc=kc, r_=r_: e.scalar_tensor_tensor(
                        out=tmp[:, r_, :], in0=xT[:, kc, tsl], scalar=vv[:, l, row, jg, kc:kc + 1], in1=rstd[:, :],
                        op0=ALU.mult, op1=ALU.mult))
                    trk.op("act", tk(r_) + [("vecs", l)], [("h", tg)], lambda e, kc=kc, r_=r_: e.activation(
                        out=hT[:, kc, tsl], in_=tmp[:, r_, :], func=AF.Identity, bias=vv[:, l, row, jsh, kc:kc + 1], scale=1.0))

        def fm_group(b, sv, skey, c0, rhs_fn, rkeys, nk_, n=512):
            maybe_mod()
            fns = [lambda e, kc=kc: e.matmul(ps[b][:, 0:n], lhsT=sv[:, kc, c0:c0 + 128], rhs=rhs_fn(kc),
                                           start=(kc == 0), stop=(kc == nk_ - 1)) for kc in range(nk_)]
            trk.group("pe", [skey] + rkeys, [("ps", b)], fns)

        def tm_group(b, sv, skey, c0, n, tt):
            maybe_mod()
            fns = [lambda e, kc=kc: e.matmul(ps[b][:, 0:n], lhsT=hT[:, kc, tt * 128:(tt + 1) * 128], rhs=sv[:, kc, c0:c0 + n],
                                           start=(kc == 0), stop=(kc == KC - 1)) for kc in range(KC)]
            trk.group("pe", [skey, ("h", tt // 4)], [("ps", b)], fns)

        def resid_update(l, row, jgate, sv, skey, s, nk_, tile_fn, rkeys_fn):
            for ct in range(4):
                oc = s * 4 + ct
                for tg in range(2):
                    b = nbank()
                    fm_group(b, sv, skey, ct * 128, lambda kc, tg=tg: tile_fn(kc, tg), rkeys_fn(tg), nk_)
                    trk.op("dve", [("ps", b), ("x", oc, tg), ("vecs", l)], [("x", oc, tg)],
                           lambda e, oc=oc, tg=tg, b=b: e.scalar_tensor_tensor(
                               out=xT[:, oc, tg * 512:(tg + 1) * 512], in0=ps[b][:, :],
                               scalar=vv[:, l, row, jgate, oc:oc + 1], in1=xT[:, oc, tg * 512:(tg + 1) * 512],
                               op0=ALU.mult, op1=ALU.add))

        def norm_chain(lane, b, gidx, out_ap, out_keys, use_rope, tt):
            c0 = lane * 256
            T0 = tmp[:, 0, c0:c0 + 256]
            T1 = tmp[:, 1, c0:c0 + 256]
            T2 = tmp[:, 2, c0:c0 + 256]
            k0, k1, k2 = ("tmp", 0, lane), ("tmp", 1, lane), ("tmp", 2, lane)
            ss = small[:, 16 + 2 * lane:18 + 2 * lane]
            ks = ("ss4", lane)
            pv = ps[b][:, 0:256].rearrange("p (h d) -> p h d", h=2)
            trk.op("act", [("ps", b)], [k0], lambda e: e.activation(out=T0, in_=ps[b][:, 0:256], func=AF.Square))
            yield
            trk.op("dve", [k0], [ks], lambda e: e.tensor_reduce(
                out=ss, in_=T0.rearrange("p (h d) -> p h d", h=2), axis=AX.X, op=ALU.add))
            yield
            trk.op("act", [ks], [ks], lambda e: e.activation(out=ss, in_=ss, func=AF.Ln, scale=1.0 / 128, bias=EPS))
            yield
            trk.op("act", [ks], [ks], lambda e: e.activation(out=ss, in_=ss, func=AF.Exp, scale=-0.5))
            yield
            t1 = T1.rearrange("p (h d) -> p h d", h=2)
            trk.op("dve", [("ps", b), ks], [k1], lambda e: e.tensor_tensor(
                out=t1, in0=pv, in1=ss.unsqueeze(2).to_broadcast([128, 2, 128]), op=ALU.mult))
            yield
            gb = gqk[:, gidx, :].unsqueeze(1).to_broadcast([128, 2, 128])
            if not use_rope:
                trk.op("dve", [k1, "gqk"], out_keys, lambda e: e.tensor_tensor(
                    out=out_ap.rearrange("p (h d) -> p h d", h=2), in0=t1, in1=gb, op=ALU.mult))
                yield
                return
            trk.op("dve", [k1, "gqk"], [k1], lambda e: e.tensor_tensor(out=t1, in0=t1, in1=gb, op=ALU.mult))
            yield
            xv = T1.rearrange("p (h a j f) -> p h a j f", h=2, a=2, j=2)
            ov = out_ap.rearrange("p (h a j f) -> p h a j f", h=2, a=2, j=2)
            cosb = ropet[:, tt, 0:64].rearrange("p (a f) -> p a f", a=2).unsqueeze(1).to_broadcast([128, 2, 2, 32])
            sinb = ropet[:, tt, 64:128].rearrange("p (a f) -> p a f", a=2).unsqueeze(1).to_broadcast([128, 2, 2, 32])
            ta = T2[:, 0:128].rearrange("p (h a f) -> p h a f", h=2, a=2)
            tb = T2[:, 128:256].rearrange("p (h a f) -> p h a f", h=2, a=2)
            x1 = xv[:, :, :, 0, :]
            x2 = xv[:, :, :, 1, :]
            trk.op("dve", [k1, "rope"], [k2], lambda e: e.tensor_tensor(out=ta, in0=x1, in1=cosb, op=ALU.mult))
            yield
            trk.op("dve", [k1, "rope"], [k2], lambda e: e.tensor_tensor(out=tb, in0=x2, in1=sinb, op=ALU.mult))
            yield
            trk.op("dve", [k2], out_keys, lambda e: e.tensor_tensor(out=ov[:, :, :, 0, :], in0=ta, in1=tb, op=ALU.subtract))
            yield
            trk.op("dve", [k1, "rope"], [k2], lambda e: e.tensor_tensor(out=ta, in0=x2, in1=cosb, op=ALU.mult))
            yield
            trk.op("dve", [k1, "rope"], [k2], lambda e: e.tensor_tensor(out=tb, in0=x1, in1=sinb, op=ALU.mult))
            yield
            trk.op("dve", [k2], out_keys, lambda e: e.tensor_tensor(out=ov[:, :, :, 1, :], in0=ta, in1=tb, op=ALU.add))
            yield

        def transpose_to(lane, dst_ap, dst_keys):
            pb = psbs[lane]
            src_bf = qtok[:, lane * 256:(lane + 1) * 256]
            fns = [lambda e, i=i: e.transpose(out=pb[:, i * 128:(i + 1) * 128], in_=src_bf[:, i * 128:(i + 1) * 128],
                                             identity=identb[:, :]) for i in range(2)]
            trk.group("pe", [("qtok", lane), "identb"], [("psb", lane)], fns)
            trk.op("act", [("psb", lane)], dst_keys, lambda e: e.activation(
                out=dst_ap, in_=pb[:, 0:256].rearrange("p (i t) -> p i t", i=2), func=AF.Copy))

        chk(0)
        for ph in _PHASES:
            row = ph
            is_s = (ph == 0)
            hstage = hT[:, :, :].rearrange("p k t -> p (k t)").bitcast(F32).rearrange("p (s c) -> p s c", s=4)
            for tt in range(NT):
                r_ = tt % 2
                trk.dma("sp", [], [("h", 0), ("h", 1), ("hst", r_)], xin_sem[r_],
                        [lambda e, tt=tt, r_=r_: e.dma_start(out=hstage[:, r_, :], in_=x_d[ph][tt * 128:(tt + 1) * 128, :])])
                for g in range(4):
                    b = nbank()
                    fns = [lambda e, g=g, i=i, r_=r_: e.transpose(
                        out=ps[b][:, i * 128:(i + 1) * 128], in_=hstage[:, r_, (g * 4 + i) * 128:(g * 4 + i + 1) * 128],
                        identity=ident) for i in range(4)]
                    trk.group("pe", [("hst", r_), "cst"], [("ps", b)], fns)
                    trk.op("act" if g % 2 else "dve", [("ps", b)], [("x", g * 4 + i, tt // 4) for i in range(4)],
                           cp("act" if g % 2 else "dve", xT[:, g * 4:(g + 1) * 4, tt * 128:(tt + 1) * 128],
                              ps[b][:, :].rearrange("p (i t) -> p i t", i=4)))

            chk(1 + ph * 100)
            for l in range(DEPTH):
                if ph == _PHASES[0] and l >= 1:
                    mod_fixups(l)
                trk.dma("sp", [], ["gqk"], ld_sem["gqk"], [
                    lambda e: e.dma_start(out=gqk[:, 0, :], in_=q_norm[l:l + 1, :].partition_broadcast(128)),
                    lambda e: e.dma_start(out=gqk[:, 1, :], in_=k_norm[l:l + 1, :].partition_broadcast(128))])
                wst = tmp[0:32, 2, :]
                for d in range(2):
                    kz = tk(2)
                    trk.op("dve", [], kz, lambda e: e.memset(wst, 0.0))
                    trk.dma("sp", [], kz, ld_sem["wg"], [
                        lambda e, d=d: e.dma_start(out=tmp[0:16, 2, :], in_=wg[d][l * 16:(l + 1) * 16, :]),
                        lambda e, d=d: e.dma_start(out=tmp[16:17, 2, :], in_=bg[d][l:l + 1, :])])
                    trk.op("dve", kz, ["wga"], lambda e, d=d: e.tensor_copy(out=wga[:, d, :], in_=wst))
                rctr["tmp"] = 0

                barrier()
                norm_mod(l, row, 1, 0)
                nkt = 10 if is_s else 8
                koff = 2 if is_s else 0
                if is_s:
                    for i in range(2):
                        trk.dma("sp", [], [("kvst", i)], ldkv_sem[i], [lambda e, i=i: e.dma_start(
                            out=kvst[:, i, :], in_=ck[l * 256 + i * 128:l * 256 + (i + 1) * 128, :])])
                        trk.op("dve", [("kvst", i)], [("qtok", i)], lambda e, i=i: e.tensor_copy(
                            out=qtok[:, i * 256:(i + 1) * 256], in_=kvst[:, i, :]))
                        transpose_to(i, kT[:, :, i * 128:(i + 1) * 128], ["kT"])
                    for i in range(2):
                        trk.dma("sp", [], [("kvst", i)], ldkv_sem[i], [lambda e, i=i: e.dma_start(
                            out=kvst[:, i, :], in_=cv[l * 256 + i * 128:l * 256 + (i + 1) * 128, :])])
                        trk.op("dve", [("kvst", i)], ["vtok"], lambda e, i=i: e.tensor_copy(out=vtok[:, i, :], in_=kvst[:, i, :]))

                def qk_slab(sv, skey, gidx, is_q, s_idx):
                    deferred = []
                    for p in range(4):
                        banks = [nbank(), nbank()]
                        tts = [2 * p, 2 * p + 1]
                        for lane in range(2):
                            tm_group(banks[lane], sv, skey, 0, 256, tts[lane])
                        for fn in deferred:
                            fn()
                        deferred = []
                        gens = []
                        for lane in range(2):
                            tt = tts[lane]
                            if is_q or is_s:
                                gens.append(norm_chain(lane, banks[lane], gidx, qtok[:, lane * 256:(lane + 1) * 256],
                                                       [("qtok", lane)], is_s, tt))
                            else:
                                gens.append(norm_chain(lane, banks[lane], gidx, kvst[:, lane, :], [("kvst", lane)], False, tt))
                        interleave(*gens)
                        for lane in range(2):
                            tt = tts[lane]
                            if (not is_q) and (not is_s):
                                seq, half = tt // 2, tt % 2
                                ro = (seq * DEPTH + l) * 256 + half * 128
                                trk.dma("sp", [("kvst", lane)], [], out_sem["k%d" % lane], [lambda e, lane=lane, ro=ro: e.dma_start(
                                    out=nk[ro:ro + 128, :], in_=kvst[:, lane, :])])
                                trk.op("dve", [("kvst", lane)], [("qtok", lane)], lambda e, lane=lane: e.tensor_copy(
                                    out=qtok[:, lane * 256:(lane + 1) * 256], in_=kvst[:, lane, :]))
                            if is_q:
                                dst = qT[:, s_idx * 2:s_idx * 2 + 2, tt * 128:(tt + 1) * 128]
                                dk = ["qT"]
                            else:
                                dst = kT[:, :, (koff + tt) * 128:(koff + tt + 1) * 128]
                                dk = ["kT"]
                            deferred.append(lambda lane=lane, dst=dst, dk=dk: transpose_to(lane, dst, dk))
                    for fn in deferred:
                        fn()

                for s in range(4):
                    sv, skey = take_slab(("qa", ph, l, s))
                    qk_slab(sv, skey, 0, True, s)
                chk(8 + ph * 100 + l * 10)
                sv, skey = take_slab(("ka", ph, l))
                qk_slab(sv, skey, 1, False, 0)
                chk(9 + ph * 100 + l * 10)
                sv, skey = take_slab(("va", ph, l))
                for tt in range(NT):
                    b = nbank()
                    tm_group(b, sv, skey, 0, 256, tt)
                    if not is_s:
                        r_ = tt % 2
                        trk.op("act", [("ps", b)], [("kvst", r_)], lambda e, r_=r_, b=b: e.activation(out=kvst[:, r_, :], in_=ps[b][:, 0:256], func=AF.Copy))
                        seq, half = tt // 2, tt % 2
                        ro = (seq * DEPTH + l) * 256 + half * 128
                        trk.dma("sp", [("kvst", r_)], [], out_sem["v%d" % r_], [lambda e, r_=r_, ro=ro: e.dma_start(
                            out=nv[ro:ro + 128, :], in_=kvst[:, r_, :])])
                        trk.op("dve", [("kvst", r_)], ["vtok"], lambda e, tt=tt, r_=r_: e.tensor_copy(out=vtok[:, koff + tt, :], in_=kvst[:, r_, :]))
                    else:
                        trk.op("dve", [("ps", b)], ["vtok"], lambda e, tt=tt, b=b: e.tensor_copy(out=vtok[:, koff + tt, :], in_=ps[b][:, 0:256]))
                chk(2 + ph * 100 + l * 10)
                scale = 128.0 ** -0.5
                if is_s:
                    jobs = [(h, tg * 512, 512, list(range(10))) for h in range(8) for tg in range(2)]
                else:
                    jobs = [(h, sq_ * 256, 256, [2 * sq_, 2 * sq_ + 1]) for sq_ in range(4) for h in range(8)]
                for (h, q0, nq, kts) in jobs:
                    kvh = h // 4
                    bo = nbank()
                    bd = nbank()

                    def score(kt):
                        b = nbank()
                        while b in (bo, bd):
                            b = nbank()
                        trk.op("pe", ["kT", "qT"], [("ps", b)], lambda e, b=b, kt=kt: e.matmul(
                            ps[b][:, 0:nq], lhsT=kT[:, kvh, kt * 128:(kt + 1) * 128], rhs=qT[:, h, q0:q0 + nq], start=True, stop=True))
                        return b

                    bnext = score(kts[0])
                    for ki, kt in enumerate(kts):
                        b = bnext
                        if ki + 1 < len(kts):
                            bnext = score(kts[ki + 1])
                        r_ = ring("sqr", 2)
                        trk.op("act", [("ps", b)], [("sqr", r_)], lambda e, b=b, r_=r_: e.activation(
                            out=pbuf[:, r_, 0:nq], in_=ps[b][:, 0:nq], func=AF.Exp, scale=scale))
                        first, last = (ki == 0), (ki == len(kts) - 1)
                        trk.group("pe", [("sqr", r_), "vtok", "onesb"], [("ps", bo), ("ps", bd)], [
                            lambda e, r_=r_, kt=kt: e.matmul(ps[bo][:, 0:nq], lhsT=vtok[:, kt, kvh * 128:(kvh + 1) * 128],
                                                             rhs=pbuf[:, r_, 0:nq], start=first, stop=last),
                            lambda e, r_=r_: e.matmul(ps[bd][:, 0:nq], lhsT=onesb[:, :], rhs=pbuf[:, r_, 0:nq], start=first, stop=last)],
                            accum=[("ps", bo), ("ps", bd)])
                    trk.op("act", [("ps", bd)], ["rstd"], lambda e: e.activation(out=rstd[:, 0:nq], in_=ps[bd][:, 0:nq], func=AF.Ln))
                    trk.op("act", ["rstd"], ["rstd"], lambda e: e.activation(out=rstd[:, 0:nq], in_=rstd[:, 0:nq], func=AF.Exp, scale=-1.0))
                    mkeys = [("m", h, q0 // 512)]
                    trk.op("dve", [("ps", bo), "rstd"], mkeys, lambda e: e.tensor_tensor(
                        out=mixT[:, h, q0:q0 + nq], in0=ps[bo][:, 0:nq], in1=rstd[:, 0:nq], op=ALU.mult))
                chk(3 + ph * 100 + l * 10)
                for s in range(4):
                    sv, skey = take_slab(("wo", ph, l, 0, s))
                    resid_update(l, row, 2, sv, skey, s, 8,
                                 lambda kc, tg: mixT[:, kc, tg * 512:(tg + 1) * 512],
                                 lambda tg: [("m", kc, tg) for kc in range(8)])

                chk(4 + ph * 100 + l * 10)
                sv, skey = take_slab(("r", ph, l))
                for d in range(2):
                    for tg in range(2):
                        b = nbank()
                        fns = [lambda e, kc=kc: e.matmul(ps[b][0:16, :], lhsT=sv[:, kc, d * 16:(d + 1) * 16],
                                                         rhs=hT[:, kc, tg * 512:(tg + 1) * 512],
                                                         start=(kc == 0), stop=(kc == KC - 1)) for kc in range(KC)]
                        trk.group("pe", [skey, ("h", tg)], [("ps", b)], fns)
                        trk.op("act", [("ps", b)], ["rTa"], lambda e, b=b, d=d, tg=tg: e.activation(
                            out=rTa[0:16, d, tg * 512:(tg + 1) * 512], in_=ps[b][0:16, :], func=AF.Copy))
                barrier()
                for hp in range(4):
                    GK = []
                    sv, skey = take_slab(("gqk", ph, l, hp))
                    for tg in range(2):
                        b = nbank()
                        fm_group(b, sv, skey, 0, lambda kc, tg=tg: hT[:, kc, tg * 512:(tg + 1) * 512], [("h", tg)], KC)
                        trk.op("act", [("ps", b)], ["qg"] + GK, lambda e, b=b, tg=tg: e.activation(
                            out=qg[:, tg * 512:(tg + 1) * 512], in_=ps[b][:, :], func=AF.Copy, scale=0.125))
                        b = nbank()
                        fm_group(b, sv, skey, 128, lambda kc, tg=tg: hT[:, kc, tg * 512:(tg + 1) * 512], [("h", tg)], KC)
                        trk.op("act", [("ps", b)], ["kgT"] + GK, lambda e, b=b, tg=tg: e.activation(
                            out=kgT[:, tg * 512:(tg + 1) * 512], in_=ps[b][:, :], func=AF.Copy))
                    for tt in range(NT):
                        b = nbank()
                        tm_group(b, sv, skey, 128, 128, tt)
                        trk.op("act", [("ps", b)], ["kgtok"] + GK, lambda e, b=b, tt=tt: e.activation(
                            out=kgtok[:, tt, :], in_=ps[b][:, 0:128], func=AF.Copy))
                    sv, skey = take_slab(("gv", ph, l, hp))
                    for tt in range(NT):
                        b = nbank()
                        tm_group(b, sv, skey, 0, 256, tt)
                        trk.op("dve", [("ps", b)], ["vgtok"] + GK, lambda e, b=b, tt=tt: e.tensor_copy(
                            out=vgtok[:, tt, :], in_=ps[b][:, 0:256]))
                    def gate_task(d, tt):
                        tsl = slice(tt * 128, (tt + 1) * 128)
                        b = nbank()
                        trk.op("pe", ["rTa", "wga"], [("ps", b)], lambda e: e.matmul(
                            ps[b][:, 0:128], lhsT=rTa[0:17, d, tsl], rhs=wga[0:17, d, hp * 128:(hp + 1) * 128], start=True, stop=True))
                        yield
                        trk.op("act", [("ps", b)], [("gpb", d)], lambda e: e.activation(
                            out=gpb[:, d, :], in_=ps[b][:, 0:128], func=AF.Exp, scale=-1.0))
                        yield
                        trk.op("act", [("gpb", d)], [("gpb", d)], lambda e: e.activation(
                            out=gpb[:, d, :], in_=gpb[:, d, :], func=AF.Ln, bias=1.0, scale=1.0))
                        yield
                        b1 = nbank()
                        trk.op("pe", [("gpb", d), "cst"], [("ps", b1)], lambda e: e.matmul(
                            ps[b1][:, 0:128], lhsT=gpb[:, d, :], rhs=U_inc[d], start=True, stop=True))
                        yield
                        b2 = nbank()
                        trk.op("pe", [("gpb", d), "cst"], [("ps", b2)], lambda e: e.matmul(
                            ps[b2][:, 0:128], lhsT=U_suf[d], rhs=gpb[:, d, :], start=True, stop=True))
                        yield
                        trk.op("act", [("ps", b1)], [("etb", d)], lambda e: e.activation(
                            out=etb[:, d, :], in_=ps[b1][:, 0:128], func=AF.Exp))
                        yield
                        trk.op("act", [("ps", b1)], [("gt", d, 0)], lambda e: e.activation(
                            out=gtmp[:, 2 * d, :], in_=ps[b1][:, 0:128], func=AF.Exp, scale=-1.0))
                        yield
                        trk.op("act", [("ps", b2)], [("gt", d, 1)], lambda e: e.activation(
                            out=gtmp[:, 2 * d + 1, :], in_=ps[b2][:, 0:128], func=AF.Exp))
                        yield
                        lc = 127 if d == 0 else 0
                        trk.op("dve", [("etb", d)], [("elast", d, tt)], lambda e: e.tensor_copy(
                            out=elast[:, d, tt:tt + 1], in_=etb[:, d, lc:lc + 1]))
                        yield
                        trk.op("dve", [("etb", d), "qg"], [("qe", d, tt)], lambda e: e.tensor_tensor(
                            out=qe[:, d, tsl], in0=qg[:, tsl], in1=etb[:, d, :], op=ALU.mult))
                        yield
                        trk.op("dve", [("gt", d, 0), "kgT"], [("ke", d, tt)], lambda e: e.tensor_tensor(
                            out=ke[:, d, tsl], in0=kgT[:, tsl], in1=gtmp[:, 2 * d, :], op=ALU.mult))
                        yield
                        trk.op("dve", [("gt", d, 1), "kgtok"], [("kd", d, tt)], lambda e: e.tensor_tensor(
                            out=kd[:, d, tt, :], in0=kgtok[:, tt, :], in1=gtmp[:, 2 * d + 1, :], op=ALU.mult))
                        yield

                    for i in range(NT):
                        interleave(gate_task(0, i), gate_task(1, NT - 1 - i))

                    def scan_task(d):
                        order = list(range(NT)) if d == 0 else list(range(NT - 1, -1, -1))
                        skeyS = ("S", d)
                        for step, tt in enumerate(order):
                            tsl = slice(tt * 128, (tt + 1) * 128)
                            if is_s:
                                if step == 0:
                                    r0 = l * 512 + hp * 128
                                    trk.dma("sp", [], [skeyS], ldst_sem[d], [lambda e: e.dma_start(
                                        out=Sst[:, d, :], in_=st_d[d][r0:r0 + 128, :])])
                                    yield
                                    trk.op("act", [skeyS], [("Sbf", d)], lambda e: e.activation(out=Sbf[:, d, :], in_=Sst[:, d, :], func=AF.Copy))
                                    yield
                            else:
                                if step % 2 == 0:
                                    trk.op("dve", [], [skeyS], lambda e: e.memset(Sst[:, d, :], 0.0))
                                    yield
                                    trk.op("dve", [], [("Sbf", d)], lambda e: e.memset(Sbf[:, d, :], 0.0))
                                    yield
                            ba = []
                            for hh in range(2):
                                rs = slice(hh * 64, (hh + 1) * 64)
                                b = nbank()
                                ba.append(b)
                                trk.op("pe", [("ke", d, tt), ("qe", d, tt)], [("ps", b)], lambda e, b=b, rs=rs: e.matmul(
                                    ps[b][:, 0:128], lhsT=ke[rs, d, tsl], rhs=qe[rs, d, tsl], start=True, stop=True))
                                yield
                            for hh in range(2):
                                b = ba[hh]
                                trk.op("dve", [("ps", b), "cst"], [("amb", d, hh)], lambda e, b=b, hh=hh: e.tensor_tensor(
                                    out=amb4[:, 2 * d + hh, :], in0=ps[b][:, 0:128], in1=maskd[d], op=ALU.mult))
                                yield
                            for hh in range(2):
                                rs = slice(hh * 64, (hh + 1) * 64)
                                bo = nbank()
                                trk.group("pe", [("Sbf", d), ("qe", d, tt), "vgtok", ("amb", d, hh)], [("ps", bo)], [
                                    lambda e, bo=bo, rs=rs: e.matmul(ps[bo][:, 0:128], lhsT=Sbf[rs, d, :], rhs=qe[rs, d, tsl], start=True, stop=False),
                                    lambda e, bo=bo, hh=hh: e.matmul(ps[bo][:, 0:128], lhsT=vgtok[:, tt, hh * 128:(hh + 1) * 128],
                                                                     rhs=amb4[:, 2 * d + hh, :], start=False, stop=True)])
                                yield
                                ok = ("oacc", hh, tt)
                                if step <= 3:
                                    trk.op("act", [("ps", bo)], [ok], lambda e, bo=bo, hh=hh: e.activation(
                                        out=oacc[:, hh, tsl], in_=ps[bo][:, 0:128], func=AF.Copy))
                                else:
                                    trk.op("dve", [("ps", bo), ok], [ok], lambda e, bo=bo, hh=hh: e.tensor_tensor(
                                        out=oacc[:, hh, tsl], in0=ps[bo][:, 0:128], in1=oacc[:, hh, tsl], op=ALU.add))
                                yield
                            bs = nbank()
                            trk.op("pe", [("kd", d, tt), "vgtok"], [("ps", bs)], lambda e, bs=bs: e.matmul(
                                ps[bs][:, 0:256], lhsT=kd[:, d, tt, :], rhs=vgtok[:, tt, :], start=True, stop=True))
                            yield
                            for hh in range(2):
                                rs = slice(hh * 64, (hh + 1) * 64)
                                trk.op("dve", [("ps", bs), skeyS, ("elast", d, tt)], [skeyS], lambda e, bs=bs, rs=rs, hh=hh: e.scalar_tensor_tensor(
                                    out=Sst[rs, d, :], in0=Sst[rs, d, :], scalar=elast[rs, d, tt:tt + 1],
                                    in1=ps[bs][rs, hh * 128:(hh + 1) * 128], op0=ALU.mult, op1=ALU.add))
                                yield
                            trk.op("act", [skeyS], [("Sbf", d)], lambda e: e.activation(out=Sbf[:, d, :], in_=Sst[:, d, :], func=AF.Copy))
                            yield
                            if (not is_s) and step % 2 == 1:
                                seq = tt // 2
                                trk.op("act", [skeyS], [("sstg", d)], lambda e: e.activation(out=sstg[:, d, :], in_=Sst[:, d, :], func=AF.Copy))
                                yield
                                ro = (seq * DEPTH + l) * 512 + hp * 128
                                trk.dma("sp", [("sstg", d)], [], out_sem["s%d" % d], [lambda e, ro=ro: e.dma_start(
                                    out=nst[d][ro:ro + 128, :], in_=sstg[:, d, :])])
                                yield

                    interleave(scan_task(0), scan_task(1))
                    sv, skey = take_slab(("go", ph, l, hp))
                    for hh in range(2):
                        for tg in range(2):
                            b = nbank()
                            fm_group(b, sv, skey, hh * 128, lambda kc, tg=tg: hT[:, kc, tg * 512:(tg + 1) * 512], [("h", tg)], KC)
                            trk.op("act", [("ps", b)], ["og"] + GK, lambda e, b=b, hh=hh, tg=tg: e.activation(
                                out=og[:, hh, tg * 512:(tg + 1) * 512], in_=ps[b][:, :], func=AF.Silu))
                    for hh in range(2):
                        for tg in range(2):
                            tsl = slice(tg * 512, (tg + 1) * 512)
                            oks = [("oacc", hh, t_i) for t_i in range(tg * 4, tg * 4 + 4)]
                            r_ = ring("sqr", 2)
                            trk.op("act", oks, [("sqr", r_)], lambda e, r_=r_, hh=hh, tsl=tsl: e.activation(
                                out=sqr[:, r_, :], in_=oacc[:, hh, tsl], func=AF.Square))
                            b = nbank()
                            trk.op("pe", [("sqr", r_), "onesb"], [("ps", b)], lambda e, b=b, r_=r_: e.matmul(
                                ps[b][:, :], lhsT=onesb[:, :], rhs=sqr[:, r_, :], start=True, stop=True))
                            rstd_from_ps(b, 512, 1.0 / 128)
                            t_ = ring("tmp", 3)
                            trk.op("dve", oks + ["rstd", "smallvecs"], tk(t_), lambda e, t_=t_, hh=hh, tsl=tsl: e.scalar_tensor_tensor(
                                out=tmp[:, t_, :], in0=oacc[:, hh, tsl], scalar=gnT[:, l:l + 1], in1=rstd[:, :], op0=ALU.mult, op1=ALU.mult))
                            trk.op("dve", tk(t_) + ["og"], [("m", 2 * hp + hh, tg)], lambda e, t_=t_, hh=hh, tsl=tsl: e.tensor_tensor(
                                out=mixT[:, 2 * hp + hh, tsl], in0=tmp[:, t_, :], in1=og[:, hh, tsl], op=ALU.mult))
                chk(5 + ph * 100 + l * 10)
                for s in range(4):
                    sv, skey = take_slab(("wo", ph, l, 1, s))
                    resid_update(l, row, 2, sv, skey, s, 8,
                                 lambda kc, tg: mixT[:, kc, tg * 512:(tg + 1) * 512],
                                 lambda tg: [("m", kc, tg) for kc in range(8)])

                chk(6 + ph * 100 + l * 10)
                norm_mod(l, row, 4, 3)
                for part in range(8):
                    for s in range(4):
                        sv, skey = take_slab(("up", ph, l, part, s))
                        for ct in range(2):
                            ft = s * 2 + ct
                            for tg in range(2):
                                tsl = slice(tg * 512, (tg + 1) * 512)
                                b = nbank()
                                fm_group(b, sv, skey, ct * 128, lambda kc, tsl=tsl: hT[:, kc, tsl], [("h", tg)], KC)
                                t_ = ring("tmp", 3)
                                trk.op("act", [("ps", b)], tk(t_), lambda e, b=b, t_=t_: e.activation(
                                    out=tmp[:, t_, :], in_=ps[b][:, :], func=AF.Relu))
                                trk.op("dve", tk(t_), [("m", ft, tg)], lambda e, t_=t_, ft=ft, tsl=tsl: e.tensor_tensor(
                                    out=mixT[:, ft, tsl], in0=tmp[:, t_, :], in1=tmp[:, t_, :], op=ALU.mult))
                    for s in range(4):
                        sv, skey = take_slab(("dn", ph, l, part, s))
                        resid_update(l, row, 5, sv, skey, s, 8,
                                     lambda kc, tg: mixT[:, kc, tg * 512:(tg + 1) * 512],
                                     lambda tg: [("m", kc, tg) for kc in range(8)])

            chk(7 + ph * 100)
            for tt in range(NT):
                r_ = tt % 2
                for g in range(4):
                    b = nbank()
                    fns = [lambda e, g=g, i=i, tt=tt: e.transpose(
                        out=ps[b][:, i * 128:(i + 1) * 128], in_=xT[:, g * 4 + i, tt * 128:(tt + 1) * 128],
                        identity=ident) for i in range(4)]
                    trk.group("pe", [("x", g * 4 + i, tt // 4) for i in range(4)] + ["cst"], [("ps", b)], fns)
                    trk.op("act" if g % 2 else "dve", [("ps", b)], [("hst", r_), ("h", 0), ("h", 1)],
                           cp("act" if g % 2 else "dve", hstage[:, r_, g * 512:(g + 1) * 512], ps[b][:, :]))
                trk.dma("sp", [("hst", r_)], [], out_sem["y%d" % r_], [lambda e, tt=tt, r_=r_: e.dma_start(
                    out=y_d[ph][tt * 128:(tt + 1) * 128, :], in_=hstage[:, r_, :])])

        for name, sem in out_sem.items():
            v = trk.dcnt.get(sem.num, 0)
            if v:
                nc.sync.wait_ge(sem, v)
        assert trk.dead or taken[0] == len(plan), (taken[0], len(plan))
    return nc


def _consts():
    s = np.arange(128)[:, None]
    t = np.arange(128)[None, :]
    c = np.zeros((128, 7 * 128), np.float32)
    c[:, 0:128] = np.eye(128, dtype=np.float32)
    c[:, 128:256] = (s <= t) * (-1.0 / 16)
    c[:, 256:384] = (s > t) * (-1.0 / 16)
    c[:, 384:512] = (s >= t) * (-1.0 / 16)
    c[:, 512:640] = (s < t) * (-1.0 / 16)
    c[:, 640:768] = (s <= t)
    c[:, 768:896] = (s >= t)
    tok = np.arange(T)
    inv = (10000.0 ** (-np.arange(32, dtype=np.float32) / 32)).astype(np.float32)
    ang_r = (tok // 64).astype(np.float32)[:, None] * inv[None, :]
    ang_c = (tok % 64).astype(np.float32)[:, None] * inv[None, :]
    rope = np.concatenate([np.cos(ang_r), np.cos(ang_c), np.sin(ang_r), np.sin(ang_c)], axis=1).astype(np.float32)
    return c, rope


_NC_CACHE = {}
_STOP_AT = None
_PHASES = (0, 1)
_NCORES = 8


def kernel(x_prompt, x_sample, cache_k, cache_v, state_gla_fwd, state_gla_bwd, c, c_ctx,
           w_mod, b_mod, norm1, w_in, q_norm, k_norm, w_gate_fwd, b_gate_fwd,
           w_gate_bwd, b_gate_bwd, gla_norm, w_out, norm2, w_up, w_down):
    f = lambda a: np.ascontiguousarray(np.asarray(a, dtype=np.float32))
    DEPTH = int(np.asarray(w_in).shape[0])
    if DEPTH not in _NC_CACHE:
        _NC_CACHE[DEPTH] = build(DEPTH, stop_at=_STOP_AT)
    nc = _NC_CACHE[DEPTH]
    cst, rope = _consts()
    x_prompt, x_sample = f(x_prompt), f(x_sample)
    cache_k, cache_v = f(cache_k), f(cache_v)
    sf, sb = f(state_gla_fwd), f(state_gla_bwd)
    c, c_ctx = f(c), f(c_ctx)
    shared = {
        "w_mod": f(w_mod).reshape(DEPTH * D, 6 * D), "b_mod": f(b_mod).reshape(DEPTH * 96, 128),
        "norm1": f(norm1).reshape(DEPTH * 16, 128), "norm2": f(norm2).reshape(DEPTH * 16, 128),
        "w_in": f(w_in).reshape(DEPTH * D, PROJ), "q_norm": f(q_norm), "k_norm": f(k_norm),
        "wgf": f(w_gate_fwd).reshape(DEPTH * 16, 512), "bgf": f(b_gate_fwd),
        "wgb": f(w_gate_bwd).reshape(DEPTH * 16, 512), "bgb": f(b_gate_bwd),
        "gla_norm": f(gla_norm), "w_out": f(w_out).reshape(DEPTH * D, D),
        "w_up": f(w_up).reshape(DEPTH * D, DFF), "w_down": f(w_down).reshape(DEPTH * DFF, D),
        "cst": cst, "rope": rope,
    }
    in_maps = []
    for core in range(8):
        b = core % 4
        m = dict(shared)
        m["xs"] = x_sample[b]
        m["xp"] = x_prompt[4 * core:4 * core + 4].reshape(T, D)
        m["ck"] = cache_k[b].reshape(DEPTH * 256, 256)
        m["cv"] = cache_v[b].reshape(DEPTH * 256, 256)
        m["sf"] = sf[b].reshape(DEPTH * 512, 128)
        m["sb"] = sb[b].reshape(DEPTH * 512, 128)
        m["cc"] = np.stack([c[b], c_ctx], 0).reshape(32, 128)
        in_maps.append(m)
    if _NCORES != 8:
        res = run_bass_kernel_spmd(nc, in_maps[:_NCORES], core_ids=list(range(_NCORES))).results
        res = [res[i % _NCORES] for i in range(8)]
    else:
        res = run_bass_kernel_spmd(nc, in_maps, core_ids=list(range(8))).results
    y_p = np.concatenate([r["yp"].reshape(4, 256, D) for r in res], 0)
    y_s = np.stack([res[b]["ys"] for b in range(4)], 0)
    nk = np.concatenate([r["nk"].reshape(4, DEPTH, 256, 2, 128) for r in res], 0)
    nv = np.concatenate([r["nv"].reshape(4, DEPTH, 256, 2, 128) for r in res], 0)
    nsf = np.concatenate([r["nsf"].reshape(4, DEPTH, 8, 64, 128) for r in res], 0)
    nsb = np.concatenate([r["nsb"].reshape(4, DEPTH, 8, 64, 128) for r in res], 0)
    return (y_p.astype(np.float32), y_s.astype(np.float32), nk.astype(np.float32), nv.astype(np.float32),
            nsf.astype(np.float32), nsb.astype(np.float32))
```

```python
import contextlib
import numpy as np
import concourse.bass as bass
import concourse.mybir as mybir
from concourse.bass_utils import run_bass_kernel_spmd

F32 = mybir.dt.float32
BF16 = mybir.dt.bfloat16
AF = mybir.ActivationFunctionType
ALU = mybir.AluOpType
AX = mybir.AxisListType

D = 2048
KC = 16
T = 1024
NT = 8
EPS = 1e-6
PROJ = 4640
DFF = 8192


class Trk:
    def __init__(self, nc, prog):
        self.nc = nc
        self.eng = {"pe": nc.tensor, "act": nc.scalar, "dve": nc.vector, "pool": nc.gpsimd, "sp": nc.sync}
        self.prog = prog
        self.cnt = {e: 0 for e in self.eng}
        self.dcnt = {}
        self.waited = {e: {} for e in self.eng}
        self.lastw = {}
        self.readers = {}
        self.final = []
        self.dead = False

    def begin(self, e, reads, writes, accum=()):
        deps = {}

        def add(tok):
            sem, val = tok
            if deps.get(sem.num, (None, 0))[1] < val:
                deps[sem.num] = (sem, val)

        own = self.prog[e].num if e in self.prog else -1
        for k in reads:
            if k in self.lastw:
                add(self.lastw[k])
        for k in writes:
            if k in self.lastw:
                tok = self.lastw[k]
                if not (k in accum and tok[0].num == own):
                    add(tok)
            for tok in self.readers.get(k, {}).values():
                add(tok)
        eng = self.eng[e]
        for num, (sem, val) in deps.items():
            if self.waited[e].get(num, 0) < val:
                eng.wait_ge(sem, val)
                self.waited[e][num] = val

    def end(self, e, inst, reads, writes, dsem=None):
        if dsem is None:
            self.cnt[e] += 1
            inst.then_inc(self.prog[e], 1)
            tok = (self.prog[e], self.cnt[e])
        else:
            self.dcnt[dsem.num] = self.dcnt.get(dsem.num, 0) + 16
            inst.then_inc(dsem, 16)
            tok = (dsem, self.dcnt[dsem.num])
        for k in reads:
            self.readers.setdefault(k, {})[tok[0].num] = tok
        for k in writes:
            self.lastw[k] = tok
            self.readers[k] = {}
        return tok

    def op(self, e, reads, writes, fn, accum=()):
        if self.dead:
            return None
        self.begin(e, reads, writes, accum)
        inst = fn(self.eng[e])
        return self.end(e, inst, reads, writes)

    def group(self, e, reads, writes, fns, accum=()):
        if self.dead:
            return None
        self.begin(e, reads, writes, accum)
        inst = None
        for fn in fns:
            inst = fn(self.eng[e])
        return self.end(e, inst, reads, writes)

    def dma(self, e, reads, writes, dsem, fns):
        if self.dead:
            return None
        self.begin(e, reads, writes)
        tok = None
        for fn in fns:
            inst = fn(self.eng[e])
            tok = self.end(e, inst, reads, writes, dsem=dsem)
        return tok


class _Stop(Exception):
    pass


def build(DEPTH, stop_at=None):
    nc = bass.Bass("TRN2", target_bir_lowering=False)

    def chk(n):
        if stop_at == n:
            trk_box[0].dead = True

    trk_box = []

    def din(name, shape):
        return nc.dram_tensor(name, list(shape), F32, kind="ExternalInput").ap()

    def dout(name, shape):
        return nc.dram_tensor(name, list(shape), F32, kind="ExternalOutput").ap()

    xs = din("xs", [T, D]); xp = din("xp", [T, D])
    ck = din("ck", [DEPTH * 256, 256]); cv = din("cv", [DEPTH * 256, 256])
    sfd = din("sf", [DEPTH * 512, 128]); sbd = din("sb", [DEPTH * 512, 128])
    cc = din("cc", [32, 128])
    w_mod = din("w_mod", [DEPTH * D, 6 * D]); b_mod = din("b_mod", [DEPTH * 96, 128])
    norm1 = din("norm1", [DEPTH * 16, 128]); norm2 = din("norm2", [DEPTH * 16, 128])
    w_in = din("w_in", [DEPTH * D, PROJ])
    q_norm = din("q_norm", [DEPTH, 128]); k_norm = din("k_norm", [DEPTH, 128])
    wg = [din("wgf", [DEPTH * 16, 512]), din("wgb", [DEPTH * 16, 512])]
    bg = [din("bgf", [DEPTH, 512]), din("bgb", [DEPTH, 512])]
    gla_norm = din("gla_norm", [DEPTH, 128])
    w_out = din("w_out", [DEPTH * D, D])
    w_up = din("w_up", [DEPTH * D, DFF]); w_down = din("w_down", [DEPTH * DFF, D])
    cst = din("cst", [128, 7 * 128]); rope = din("rope", [T, 128])
    y_d = [dout("ys", [T, D]), dout("yp", [T, D])]
    nk = dout("nk", [4 * DEPTH * 256, 256]); nv = dout("nv", [4 * DEPTH * 256, 256])
    nst = [dout("nsf", [4 * DEPTH * 512, 128]), dout("nsb", [4 * DEPTH * 512, 128])]
    x_d = [xs, xp]
    st_d = [sfd, sbd]

    es = contextlib.ExitStack()
    with es:
        def sb_(name, shape, dt):
            return es.enter_context(nc.sbuf_tensor(name, list(shape), dt))

        def sem_(name):
            return es.enter_context(nc.semaphore(name))

        xT = sb_("xT", [128, KC, T], F32)
        hT = sb_("hT", [128, KC, T], BF16)
        mixT = sb_("mixT", [128, 8, T], BF16)
        slab = sb_("slab", [128, 2, 4096], BF16)
        scr = sb_("scr", [128, 7680], F32)
        cstt = sb_("cstt", [128, 7 * 128], F32)
        identb = sb_("identb", [128, 128], BF16)
        onesb = sb_("onesb", [128, 128], BF16)
        ropet = sb_("ropet", [128, NT, 128], F32)
        vecs = sb_("vecs", [128, DEPTH * 2 * 6 * 16], F32)
        n1T = sb_("n1T", [128, DEPTH * 16], F32)
        n2T = sb_("n2T", [128, DEPTH * 16], F32)
        gnT = sb_("gnT", [128, DEPTH], F32)
        scT = sb_("scT", [128, 32], BF16)
        gqk = sb_("gqk", [128, 2, 128], F32)
        wga = sb_("wga", [32, 2, 512], BF16)
        rTa = sb_("rTa", [32, 2, T], BF16)
        tmp = sb_("tmp", [128, 3, 512], F32)
        sqr = sb_("sqr", [128, 2, 512], BF16)
        pbuf = sqr
        kvst = sb_("kvst", [128, 2, 256], F32)
        sstg = sb_("sstg", [128, 2, 128], F32)
        rstd = sb_("rstd", [128, 512], F32)
        small = sb_("small", [128, 64], F32)
        ps = [es.enter_context(nc.psum_tensor("ps%d" % i, [128, 512], F32)) for i in range(6)]
        psbs = [es.enter_context(nc.psum_tensor("psb%d" % i, [128, 1024], BF16)) for i in range(2)]
        gtmp = sb_("gtmp", [128, 4, 128], F32)
        amb4 = sb_("amb4", [128, 4, 128], BF16)

        prog = {e: sem_("p_" + e) for e in ("pe", "act", "dve", "pool", "sp")}
        sl_sem = [sem_("slab0"), sem_("slab1")]
        misc_sem = sem_("misc")
        cst_sem = sem_("cst")
        rope_sem = sem_("rope")
        xin_sem = [sem_("xin0"), sem_("xin1")]
        ld_sem = {n: sem_("ld_" + n) for n in ("gqk", "wg")}
        ldkv_sem = [sem_("ldkv0"), sem_("ldkv1")]
        ldst_sem = [sem_("ldst0"), sem_("ldst1")]
        out_sem = {n: sem_("o_" + n) for n in ("y0", "y1", "k0", "k1", "v0", "v1", "s0", "s1")}

        trk = Trk(nc, prog)
        trk_box.append(trk)
        ident = cstt[:, 0:128]
        U_inc = [cstt[:, 128:256], cstt[:, 384:512]]
        U_suf = [cstt[:, 256:384], cstt[:, 512:640]]
        maskd = [cstt[:, 640:768], cstt[:, 768:896]]

        bmT = sb_("bmT", [128, DEPTH * 96], F32)[:, :]
        sbf = scr[:, :].bitcast(BF16)
        qT = sbf[:, 0:8192].rearrange("p (h t) -> p h t", h=8)
        kT = sbf[:, 8192:10752].rearrange("p (h t) -> p h t", h=2)
        vtok = sbf[:, 10752:13312].rearrange("p (k c) -> p k c", k=10)
        qtok = sbf[:, 13312:13824]
        qg = sbf[:, 0:1024]
        kgT = sbf[:, 1024:2048]
        vgtok = sbf[:, 2048:4096].rearrange("p (k c) -> p k c", k=NT)
        qe = sbf[:, 4096:6144].rearrange("p (d t) -> p d t", d=2)
        ke = sbf[:, 6144:8192].rearrange("p (d t) -> p d t", d=2)
        kd = sbf[:, 8192:10240].rearrange("p (d k c) -> p d k c", d=2, k=NT)
        og = sbf[:, 10240:12288].rearrange("p (h t) -> p h t", h=2)
        Sbf = sbf[:, 12288:12544].rearrange("p (d c) -> p d c", d=2)
        amb = sbf[:, 12544:12800].rearrange("p (r c) -> p r c", r=2)
        kgtok = scr[:, 6400:7424].rearrange("p (k c) -> p k c", k=NT)
        Sst = scr[:, 7424:7680].rearrange("p (d c) -> p d c", d=2)
        oacc = sb_("oacc", [128, 2, T], F32)
        elast = small[:, 0:16].rearrange("p (d k) -> p d k", d=2)
        gpb = sb_("gpb", [128, 2, 128], F32)
        etb = sb_("etb", [128, 2, 128], F32)

        pctr = [0]

        def nbank():
            i = pctr[0] % 6
            pctr[0] += 1
            return i

        rctr = {}

        def tk(r):
            return [("tmp", r, 0), ("tmp", r, 1)]

        def interleave_w(lanes):
            lanes = [[g, st] for g, st in lanes]
            rnd = 0
            while lanes:
                if all(ln[1] > 1 for ln in lanes):
                    for ln in lanes:
                        ln[1] = 1
                for ln in list(lanes):
                    if rnd % ln[1] == 0:
                        try:
                            next(ln[0])
                        except StopIteration:
                            lanes.remove(ln)
                rnd += 1

        def interleave(*gens):
            gens = list(gens)
            while gens:
                for g in list(gens):
                    try:
                        next(g)
                    except StopIteration:
                        gens.remove(g)

        def ring(name, n):
            i = rctr.get(name, 0)
            rctr[name] = i + 1
            return i % n

        plan = []
        issued = [0]
        taken = [0]
        slot_of = {}
        free_slots = [0, 1]
        cur_reg = [None]
        grp = [0]
        hook_at = [2]
        HOOK = {"qa": 5, "ka": 5, "va": 5, "wo": 5, "gqk": 7, "gv": 5, "go": 3, "up": 3, "dn": 5}
        modcnt = {}

        def slab_view(i):
            spec = plan[i]
            slot = slot_of[i]
            view = slab[:, slot, 0:spec["kcs"] * spec["ncols"]].rearrange("p (k n) -> p k n", k=spec["kcs"])
            return view, ("slab", slot)

        def issue_ready():
            while issued[0] < len(plan) and free_slots:
                i = issued[0]
                slot = free_slots.pop(0)
                slot_of[i] = slot
                spec = plan[i]
                view, _ = slab_view(i)
                fns = []
                for (src, c0, n) in spec["parts"]:
                    srcv = src.rearrange("(k p) n -> p k n", p=128)
                    fns.append(lambda eng, srcv=srcv, c0=c0, n=n, view=view: eng.dma_start(
                        out=view[:, :, c0:c0 + n], in_=srcv, max_dma_last_dim=2048))
                trk.dma("pool", [], [("slab", slot)], sl_sem[slot], fns)
                issued[0] += 1

        def release(i):
            free_slots.append(slot_of[i])
            issue_ready()

        def take_mod():
            i = taken[0]
            tag = plan[i]["tag"]
            assert tag[0] == "mod"
            issue_ready()
            assert i in slot_of
            taken[0] += 1
            sv, skey = slab_view(i)
            consume_mod(tag[1], tag[2], sv, skey)
            release(i)

        def take_slab(tag):
            while plan[taken[0]]["tag"][0] == "mod":
                take_mod()
            i = taken[0]
            assert plan[i]["tag"] == tag, (plan[i]["tag"], tag)
            if cur_reg[0] is not None:
                release(cur_reg[0])
            issue_ready()
            assert i in slot_of
            taken[0] += 1
            cur_reg[0] = i
            grp[0] = 0
            hook_at[0] = HOOK.get(tag[0], 2)
            return slab_view(i)

        def maybe_mod():
            grp[0] += 1
            if grp[0] == hook_at[0] and taken[0] < len(plan) and plan[taken[0]]["tag"][0] == "mod":
                take_mod()

        def add_plan(tag, kcs, ncols, parts):
            plan.append(dict(tag=tag, kcs=kcs, ncols=ncols, parts=parts))

        def mod_spec(l, s):
            return dict(tag=("mod", l, s), kcs=16, ncols=256,
                        parts=[(w_mod[l * D:(l + 1) * D, s * 256:(s + 1) * 256], 0, 256)])

        for s in range(48):
            plan.append(mod_spec(0, s))
        n_pre = len(plan)
        later_mods = [mod_spec(l, s) for l in range(1, DEPTH) for s in range(48)]
        for ph in _PHASES:
            for l in range(DEPTH):
                wl = w_in[l * D:(l + 1) * D, :]
                for s in range(4):
                    add_plan(("qa", ph, l, s), 16, 256, [(wl[:, s * 256:(s + 1) * 256], 0, 256)])
                add_plan(("ka", ph, l), 16, 256, [(wl[:, 1024:1280], 0, 256)])
                add_plan(("va", ph, l), 16, 256, [(wl[:, 1280:1536], 0, 256)])
                for s in range(4):
                    add_plan(("wo", ph, l, 0, s), 8, 512, [(w_out[l * D:l * D + 1024, s * 512:(s + 1) * 512], 0, 512)])
                add_plan(("r", ph, l), 16, 32, [(wl[:, 4608:4640], 0, 32)])
                for hp in range(4):
                    add_plan(("gqk", ph, l, hp), 16, 256, [(wl[:, 1536 + hp * 128:1536 + (hp + 1) * 128], 0, 128),
                                                            (wl[:, 2048 + hp * 128:2048 + (hp + 1) * 128], 128, 128)])
                    add_plan(("gv", ph, l, hp), 16, 256, [(wl[:, 2560 + hp * 256:2560 + (hp + 1) * 256], 0, 256)])
                    add_plan(("go", ph, l, hp), 16, 256, [(wl[:, 3584 + hp * 256:3584 + (hp + 1) * 256], 0, 256)])
                for s in range(4):
                    add_plan(("wo", ph, l, 1, s), 8, 512, [(w_out[l * D + 1024:(l + 1) * D, s * 512:(s + 1) * 512], 0, 512)])
                for part in range(8):
                    for s in range(4):
                        c0 = part * 1024 + s * 256
                        add_plan(("up", ph, l, part, s), 16, 256, [(w_up[l * D:(l + 1) * D, c0:c0 + 256], 0, 256)])
                    for s in range(4):
                        r0 = l * DFF + part * 1024
                        add_plan(("dn", ph, l, part, s), 8, 512, [(w_down[r0:r0 + 1024, s * 512:(s + 1) * 512], 0, 512)])

        if later_mods:
            regs = plan[n_pre:]
            newp = plan[:n_pre]
            for r_ in regs:
                newp.append(r_)
                if later_mods:
                    newp.append(later_mods.pop(0))
            assert not later_mods
            plan[:] = newp

        def loadT(dst, src, n, eng="dve"):
            st = tmp[0:n, 0, 0:128]
            trk.dma("sp", [], tk(0), misc_sem, [lambda e: e.dma_start(out=st, in_=src)])
            b = 6 if False else nbank()
            trk.op("pe", tk(0) + ["cst"], [("ps", b)],
                   lambda e: e.transpose(out=ps[b][:, 0:n], in_=st, identity=ident[0:n, 0:n]))
            trk.op(eng, [("ps", b)], [dst_key(dst)],
                   lambda e: e.tensor_copy(out=dst, in_=ps[b][:, 0:n]))

        def dst_key(ap):
            return "smallvecs"

        def barrier():
            if trk.dead:
                return
            for e in ("pe", "act", "dve"):
                for o in ("pe", "act", "dve"):
                    if o != e and trk.waited[e].get(prog[o].num, 0) < trk.cnt[o]:
                        trk.eng[e].wait_ge(prog[o], trk.cnt[o])
                        trk.waited[e][prog[o].num] = trk.cnt[o]

        def cp(en, out, in_):
            if en == "act":
                return lambda e: e.activation(out=out, in_=in_, func=AF.Copy)
            return lambda e: e.tensor_copy(out=out, in_=in_)

        trk.dma("sp", [], ["cst"], cst_sem, [lambda e: e.dma_start(out=cstt[:, :], in_=cst)])
        trk.dma("sp", [], ["rope"], rope_sem,
                [lambda e: e.dma_start(out=ropet[:, :, :], in_=rope.rearrange("(k p) c -> p k c", p=128))])
        trk.op("dve", ["cst"], ["identb"], lambda e: e.tensor_copy(out=identb[:, :], in_=ident))
        trk.op("dve", [], ["onesb"], lambda e: e.memset(onesb[:, :], 1.0))
        trk.op("dve", [], ["rTa"], lambda e: e.memset(rTa[:, :, :], 1.0))
        loadT(n1T[:, :], norm1, DEPTH * 16)
        loadT(n2T[:, :], norm2, DEPTH * 16)
        loadT(gnT[:, :], gla_norm, DEPTH)
        for l in range(DEPTH):
            loadT(bmT[:, l * 96:(l + 1) * 96], b_mod[l * 96:(l + 1) * 96, :], 96)
        loadT(small[:, 32:64], cc, 32)
        trk.op("act", ["smallvecs"], ["scT"],
               lambda e: e.activation(out=scT[:, :], in_=small[:, 32:64], func=AF.Silu))

        bmv = bmT.rearrange("p (l c) -> p l c", l=DEPTH)
        scv = scT[:, :].rearrange("p (r k) -> p k r", r=2)
        vv = vecs[:, :].rearrange("p (l r j k) -> p l r j k", l=DEPTH, r=2, j=6)

        def consume_mod(l, s, sv, skey):
            b = nbank()
            fns = []
            for ct in range(2):
                for kc in range(KC):
                    fns.append(lambda e, ct=ct, kc=kc: e.matmul(
                        ps[b][:, ct * 2:ct * 2 + 2], lhsT=sv[:, kc, ct * 128:(ct + 1) * 128], rhs=scv[:, kc, :],
                        start=(kc == 0), stop=(kc == KC - 1)))
            trk.group("pe", [skey, "scT"], [("ps", b)], fns)
            j, k0 = s // 8, 2 * (s % 8)
            trk.op("dve", [("ps", b), "smallvecs"], [("vecs", l)],
                   lambda e: e.tensor_tensor(
                       out=vv[:, l, :, j, k0:k0 + 2],
                       in0=ps[b][:, 0:4].rearrange("p (c r) -> p r c", r=2),
                       in1=bmv[:, l, s * 2:s * 2 + 2].unsqueeze(1).to_broadcast([128, 2, 2]), op=ALU.add))
            modcnt[l] = modcnt.get(l, 0) + 1

        def mod_fixups(l):
            if trk.dead:
                return
            assert modcnt.get(l, 0) == 48, (l, modcnt)
            for r in range(2):
                for j, nrm in ((1, n1T), (4, n2T)):
                    trk.op("dve", [("vecs", l), "smallvecs"], [("vecs", l)],
                           lambda e, r=r, j=j, nrm=nrm: e.scalar_tensor_tensor(
                               out=vv[:, l, r, j, :], in0=vv[:, l, r, j, :], scalar=1.0, in1=nrm[:, l * 16:(l + 1) * 16],
                               op0=ALU.add, op1=ALU.mult))

        for s in range(48):
            take_mod()
        mod_fixups(0)

        def rstd_from_ps(b, n, inv_n):
            trk.op("act", [("ps", b)], ["rstd"], lambda e: e.activation(
                out=rstd[:, 0:n], in_=ps[b][:, 0:n], func=AF.Ln, scale=inv_n, bias=EPS))
            trk.op("act", ["rstd"], ["rstd"], lambda e: e.activation(out=rstd[:, 0:n], in_=rstd[:, 0:n], func=AF.Exp, scale=-0.5))

        def norm_mod(l, row, jg, jsh):
            for tg in range(2):
                tsl = slice(tg * 512, (tg + 1) * 512)
                b = nbank()
                for kc in range(KC):
                    r_ = ring("sqr", 2)
                    trk.op("act", [("x", kc, tg)], [("sqr", r_)], lambda e, kc=kc, r_=r_: e.activation(
                        out=sqr[:, r_, :], in_=xT[:, kc, tsl], func=AF.Square))
                    trk.op("pe", [("sqr", r_), "onesb"], [("ps", b)], lambda e, kc=kc, r_=r_: e.matmul(
                        ps[b][:, :], lhsT=onesb[:, :], rhs=sqr[:, r_, :], start=(kc == 0), stop=(kc == KC - 1)),
                        accum=[("ps", b)])
                rstd_from_ps(b, 512, 1.0 / D)
                for kc in range(KC):
                    r_ = ring("tmp", 3)
                    trk.op("dve", [("x", kc, tg), "rstd", ("vecs", l)], tk(r_), lambda e, kc=kc, r_=r_: e.scalar_tensor_tensor(
                        out=tmp[:, r_, :], in0=xT[:, kc, tsl], scalar=vv[:, l, row, jg, kc:kc + 1], in1=rstd[:, :],
                        op0=ALU.mult, op1=ALU.mult))
                    trk.op("act", tk(r_) + [("vecs", l)], [("h", tg)], lambda e, kc=kc, r_=r_: e.activation(
                        out=hT[:, kc, tsl], in_=tmp[:, r_, :], func=AF.Identity, bias=vv[:, l, row, jsh, kc:kc + 1], scale=1.0))

        def fm_group(b, sv, skey, c0, rhs_fn, rkeys, nk_, n=512):
            maybe_mod()
            fns = [lambda e, kc=kc: e.matmul(ps[b][:, 0:n], lhsT=sv[:, kc, c0:c0 + 128], rhs=rhs_fn(kc),
                                           start=(kc == 0), stop=(kc == nk_ - 1)) for kc in range(nk_)]
            trk.group("pe", [skey] + rkeys, [("ps", b)], fns)

        def tm_group(b, sv, skey, c0, n, tt):
            maybe_mod()
            fns = [lambda e, kc=kc: e.matmul(ps[b][:, 0:n], lhsT=hT[:, kc, tt * 128:(tt + 1) * 128], rhs=sv[:, kc, c0:c0 + n],
                                           start=(kc == 0), stop=(kc == KC - 1)) for kc in range(KC)]
            trk.group("pe", [skey, ("h", tt // 4)], [("ps", b)], fns)

        def resid_update(l, row, jgate, sv, skey, s, nk_, tile_fn, rkeys_fn):
            for ct in range(4):
                oc = s * 4 + ct
                for tg in range(2):
                    b = nbank()
                    fm_group(b, sv, skey, ct * 128, lambda kc, tg=tg: tile_fn(kc, tg), rkeys_fn(tg), nk_)
                    trk.op("dve", [("ps", b), ("x", oc, tg), ("vecs", l)], [("x", oc, tg)],
                           lambda e, oc=oc, tg=tg, b=b: e.scalar_tensor_tensor(
                               out=xT[:, oc, tg * 512:(tg + 1) * 512], in0=ps[b][:, :],
                               scalar=vv[:, l, row, jgate, oc:oc + 1], in1=xT[:, oc, tg * 512:(tg + 1) * 512],
                               op0=ALU.mult, op1=ALU.add))

        def norm_chain(lane, b, gidx, out_ap, out_keys, use_rope, tt):
            c0 = lane * 256
            T0 = tmp[:, 0, c0:c0 + 256]
            T1 = tmp[:, 1, c0:c0 + 256]
            T2 = tmp[:, 2, c0:c0 + 256]
            k0, k1, k2 = ("tmp", 0, lane), ("tmp", 1, lane), ("tmp", 2, lane)
            ss = small[:, 16 + 2 * lane:18 + 2 * lane]
            ks = ("ss4", lane)
            pv = ps[b][:, 0:256].rearrange("p (h d) -> p h d", h=2)
            trk.op("act", [("ps", b)], [k0], lambda e: e.activation(out=T0, in_=ps[b][:, 0:256], func=AF.Square))
            yield
            trk.op("dve", [k0], [ks], lambda e: e.tensor_reduce(
                out=ss, in_=T0.rearrange("p (h d) -> p h d", h=2), axis=AX.X, op=ALU.add))
            yield
            trk.op("act", [ks], [ks], lambda e: e.activation(out=ss, in_=ss, func=AF.Ln, scale=1.0 / 128, bias=EPS))
            yield
            trk.op("act", [ks], [ks], lambda e: e.activation(out=ss, in_=ss, func=AF.Exp, scale=-0.5))
            yield
            t1 = T1.rearrange("p (h d) -> p h d", h=2)
            trk.op("dve", [("ps", b), ks], [k1], lambda e: e.tensor_tensor(
                out=t1, in0=pv, in1=ss.unsqueeze(2).to_broadcast([128, 2, 128]), op=ALU.mult))
            yield
            gb = gqk[:, gidx, :].unsqueeze(1).to_broadcast([128, 2, 128])
            if not use_rope:
                trk.op("dve", [k1, "gqk"], out_keys, lambda e: e.tensor_tensor(
                    out=out_ap.rearrange("p (h d) -> p h d", h=2), in0=t1, in1=gb, op=ALU.mult))
                yield
                return
            trk.op("dve", [k1, "gqk"], [k1], lambda e: e.tensor_tensor(out=t1, in0=t1, in1=gb, op=ALU.mult))
            yield
            xv = T1.rearrange("p (h a j f) -> p h a j f", h=2, a=2, j=2)
            ov = out_ap.rearrange("p (h a j f) -> p h a j f", h=2, a=2, j=2)
            cosb = ropet[:, tt, 0:64].rearrange("p (a f) -> p a f", a=2).unsqueeze(1).to_broadcast([128, 2, 2, 32])
            sinb = ropet[:, tt, 64:128].rearrange("p (a f) -> p a f", a=2).unsqueeze(1).to_broadcast([128, 2, 2, 32])
            ta = T2[:, 0:128].rearrange("p (h a f) -> p h a f", h=2, a=2)
            tb = T2[:, 128:256].rearrange("p (h a f) -> p h a f", h=2, a=2)
            x1 = xv[:, :, :, 0, :]
            x2 = xv[:, :, :, 1, :]
            trk.op("dve", [k1, "rope"], [k2], lambda e: e.tensor_tensor(out=ta, in0=x1, in1=cosb, op=ALU.mult))
            yield
            trk.op("dve", [k1, "rope"], [k2], lambda e: e.tensor_tensor(out=tb, in0=x2, in1=sinb, op=ALU.mult))
            yield
            trk.op("dve", [k2], out_keys, lambda e: e.tensor_tensor(out=ov[:, :, :, 0, :], in0=ta, in1=tb, op=ALU.subtract))
            yield
            trk.op("dve", [k1, "rope"], [k2], lambda e: e.tensor_tensor(out=ta, in0=x2, in1=cosb, op=ALU.mult))
            yield
            trk.op("dve", [k1, "rope"], [k2], lambda e: e.tensor_tensor(out=tb, in0=x1, in1=sinb, op=ALU.mult))
            yield
            trk.op("dve", [k2], out_keys, lambda e: e.tensor_tensor(out=ov[:, :, :, 1, :], in0=ta, in1=tb, op=ALU.add))
            yield

        def transpose_to(lane, dst_ap, dst_keys):
            pb = psbs[lane]
            src_bf = qtok[:, lane * 256:(lane + 1) * 256]
            fns = [lambda e, i=i: e.transpose(out=pb[:, i * 128:(i + 1) * 128], in_=src_bf[:, i * 128:(i + 1) * 128],
                                             identity=identb[:, :]) for i in range(2)]
            trk.group("pe", [("qtok", lane), "identb"], [("psb", lane)], fns)
            trk.op("act", [("psb", lane)], dst_keys, lambda e: e.activation(
                out=dst_ap, in_=pb[:, 0:256].rearrange("p (i t) -> p i t", i=2), func=AF.Copy))

        chk(0)
        for ph in _PHASES:
            row = ph
            is_s = (ph == 0)
            hstage = hT[:, :, :].rearrange("p k t -> p (k t)").bitcast(F32).rearrange("p (s c) -> p s c", s=4)
            for tt in range(NT):
                r_ = tt % 2
                trk.dma("sp", [], [("h", 0), ("h", 1), ("hst", r_)], xin_sem[r_],
                        [lambda e, tt=tt, r_=r_: e.dma_start(out=hstage[:, r_, :], in_=x_d[ph][tt * 128:(tt + 1) * 128, :])])
                for g in range(4):
                    b = nbank()
                    fns = [lambda e, g=g, i=i, r_=r_: e.transpose(
                        out=ps[b][:, i * 128:(i + 1) * 128], in_=hstage[:, r_, (g * 4 + i) * 128:(g * 4 + i + 1) * 128],
                        identity=ident) for i in range(4)]
                    trk.group("pe", [("hst", r_), "cst"], [("ps", b)], fns)
                    trk.op("act" if g % 2 else "dve", [("ps", b)], [("x", g * 4 + i, tt // 4) for i in range(4)],
                           cp("act" if g % 2 else "dve", xT[:, g * 4:(g + 1) * 4, tt * 128:(tt + 1) * 128],
                              ps[b][:, :].rearrange("p (i t) -> p i t", i=4)))

            chk(1 + ph * 100)
            for l in range(DEPTH):
                if ph == _PHASES[0] and l >= 1:
                    mod_fixups(l)
                trk.dma("sp", [], ["gqk"], ld_sem["gqk"], [
                    lambda e: e.dma_start(out=gqk[:, 0, :], in_=q_norm[l:l + 1, :].partition_broadcast(128)),
                    lambda e: e.dma_start(out=gqk[:, 1, :], in_=k_norm[l:l + 1, :].partition_broadcast(128))])
                wst = tmp[0:32, 2, :]
                for d in range(2):
                    kz = tk(2)
                    trk.op("dve", [], kz, lambda e: e.memset(wst, 0.0))
                    trk.dma("sp", [], kz, ld_sem["wg"], [
                        lambda e, d=d: e.dma_start(out=tmp[0:16, 2, :], in_=wg[d][l * 16:(l + 1) * 16, :]),
                        lambda e, d=d: e.dma_start(out=tmp[16:17, 2, :], in_=bg[d][l:l + 1, :])])
                    trk.op("dve", kz, ["wga"], lambda e, d=d: e.tensor_copy(out=wga[:, d, :], in_=wst))
                rctr["tmp"] = 0

                barrier()
                norm_mod(l, row, 1, 0)
                nkt = 10 if is_s else 8
                koff = 2 if is_s else 0
                if is_s:
                    for i in range(2):
                        trk.dma("sp", [], [("kvst", i)], ldkv_sem[i], [lambda e, i=i: e.dma_start(
                            out=kvst[:, i, :], in_=ck[l * 256 + i * 128:l * 256 + (i + 1) * 128, :])])
                        trk.op("dve", [("kvst", i)], [("qtok", i)], lambda e, i=i: e.tensor_copy(
                            out=qtok[:, i * 256:(i + 1) * 256], in_=kvst[:, i, :]))
                        transpose_to(i, kT[:, :, i * 128:(i + 1) * 128], ["kT"])
                    for i in range(2):
                        trk.dma("sp", [], [("kvst", i)], ldkv_sem[i], [lambda e, i=i: e.dma_start(
                            out=kvst[:, i, :], in_=cv[l * 256 + i * 128:l * 256 + (i + 1) * 128, :])])
                        trk.op("dve", [("kvst", i)], ["vtok"], lambda e, i=i: e.tensor_copy(out=vtok[:, i, :], in_=kvst[:, i, :]))

                def qk_slab(sv, skey, gidx, is_q, s_idx):
                    deferred = []
                    for p in range(4):
                        banks = [nbank(), nbank()]
                        tts = [2 * p, 2 * p + 1]
                        for lane in range(2):
                            tm_group(banks[lane], sv, skey, 0, 256, tts[lane])
                        for fn in deferred:
                            fn()
                        deferred = []
                        gens = []
                        for lane in range(2):
                            tt = tts[lane]
                            if is_q or is_s:
                                gens.append(norm_chain(lane, banks[lane], gidx, qtok[:, lane * 256:(lane + 1) * 256],
                                                       [("qtok", lane)], is_s, tt))
                            else:
                                gens.append(norm_chain(lane, banks[lane], gidx, kvst[:, lane, :], [("kvst", lane)], False, tt))
                        interleave(*gens)
                        for lane in range(2):
                            tt = tts[lane]
                            if (not is_q) and (not is_s):
                                seq, half = tt // 2, tt % 2
                                ro = (seq * DEPTH + l) * 256 + half * 128
                                trk.dma("sp", [("kvst", lane)], [], out_sem["k%d" % lane], [lambda e, lane=lane, ro=ro: e.dma_start(
                                    out=nk[ro:ro + 128, :], in_=kvst[:, lane, :])])
                                trk.op("dve", [("kvst", lane)], [("qtok", lane)], lambda e, lane=lane: e.tensor_copy(
                                    out=qtok[:, lane * 256:(lane + 1) * 256], in_=kvst[:, lane, :]))
                            if is_q:
                                dst = qT[:, s_idx * 2:s_idx * 2 + 2, tt * 128:(tt + 1) * 128]
                                dk = ["qT"]
                            else:
                                dst = kT[:, :, (koff + tt) * 128:(koff + tt + 1) * 128]
                                dk = ["kT"]
                            deferred.append(lambda lane=lane, dst=dst, dk=dk: transpose_to(lane, dst, dk))
                    for fn in deferred:
                        fn()

                for s in range(4):
                    sv, skey = take_slab(("qa", ph, l, s))
                    qk_slab(sv, skey, 0, True, s)
                chk(8 + ph * 100 + l * 10)
                sv, skey = take_slab(("ka", ph, l))
                qk_slab(sv, skey, 1, False, 0)
                chk(9 + ph * 100 + l * 10)
                sv, skey = take_slab(("va", ph, l))
                for tt in range(NT):
                    b = nbank()
                    tm_group(b, sv, skey, 0, 256, tt)
                    if not is_s:
                        r_ = tt % 2
                        trk.op("act", [("ps", b)], [("kvst", r_)], lambda e, r_=r_, b=b: e.activation(out=kvst[:, r_, :], in_=ps[b][:, 0:256], func=AF.Copy))
                        seq, half = tt // 2, tt % 2
                        ro = (seq * DEPTH + l) * 256 + half * 128
                        trk.dma("sp", [("kvst", r_)], [], out_sem["v%d" % r_], [lambda e, r_=r_, ro=ro: e.dma_start(
                            out=nv[ro:ro + 128, :], in_=kvst[:, r_, :])])
                        trk.op("dve", [("kvst", r_)], ["vtok"], lambda e, tt=tt, r_=r_: e.tensor_copy(out=vtok[:, koff + tt, :], in_=kvst[:, r_, :]))
                    else:
                        trk.op("dve", [("ps", b)], ["vtok"], lambda e, tt=tt, b=b: e.tensor_copy(out=vtok[:, koff + tt, :], in_=ps[b][:, 0:256]))
                chk(2 + ph * 100 + l * 10)
                scale = 128.0 ** -0.5
                if is_s:
                    jobs = [(h, tg * 512, 512, list(range(10))) for h in range(8) for tg in range(2)]
                else:
                    jobs = [(h, sq_ * 256, 256, [2 * sq_, 2 * sq_ + 1]) for sq_ in range(4) for h in range(8)]
                for (h, q0, nq, kts) in jobs:
                    kvh = h // 4
                    bo = nbank()
                    bd = nbank()

                    def score(kt):
                        b = nbank()
                        while b in (bo, bd):
                            b = nbank()
                        trk.op("pe", ["kT", "qT"], [("ps", b)], lambda e, b=b, kt=kt: e.matmul(
                            ps[b][:, 0:nq], lhsT=kT[:, kvh, kt * 128:(kt + 1) * 128], rhs=qT[:, h, q0:q0 + nq], start=True, stop=True))
                        return b

                    bnext = score(kts[0])
                    for ki, kt in enumerate(kts):
                        b = bnext
                        if ki + 1 < len(kts):
                            bnext = score(kts[ki + 1])
                        r_ = ring("sqr", 2)
                        trk.op("act", [("ps", b)], [("sqr", r_)], lambda e, b=b, r_=r_: e.activation(
                            out=pbuf[:, r_, 0:nq], in_=ps[b][:, 0:nq], func=AF.Exp, scale=scale))
                        first, last = (ki == 0), (ki == len(kts) - 1)
                        trk.group("pe", [("sqr", r_), "vtok", "onesb"], [("ps", bo), ("ps", bd)], [
                            lambda e, r_=r_, kt=kt: e.matmul(ps[bo][:, 0:nq], lhsT=vtok[:, kt, kvh * 128:(kvh + 1) * 128],
                                                             rhs=pbuf[:, r_, 0:nq], start=first, stop=last),
                            lambda e, r_=r_: e.matmul(ps[bd][:, 0:nq], lhsT=onesb[:, :], rhs=pbuf[:, r_, 0:nq], start=first, stop=last)],
                            accum=[("ps", bo), ("ps", bd)])
                    trk.op("act", [("ps", bd)], ["rstd"], lambda e: e.activation(out=rstd[:, 0:nq], in_=ps[bd][:, 0:nq], func=AF.Ln))
                    trk.op("act", ["rstd"], ["rstd"], lambda e: e.activation(out=rstd[:, 0:nq], in_=rstd[:, 0:nq], func=AF.Exp, scale=-1.0))
                    mkeys = [("m", h, q0 // 512)]
                    trk.op("dve", [("ps", bo), "rstd"], mkeys, lambda e: e.tensor_tensor(
                        out=mixT[:, h, q0:q0 + nq], in0=ps[bo][:, 0:nq], in1=rstd[:, 0:nq], op=ALU.mult))
                chk(3 + ph * 100 + l * 10)
                for s in range(4):
                    sv, skey = take_slab(("wo", ph, l, 0, s))
                    resid_update(l, row, 2, sv, skey, s, 8,
                                 lambda kc, tg: mixT[:, kc, tg * 512:(tg + 1) * 512],
                                 lambda tg: [("m", kc, tg) for kc in range(8)])

                chk(4 + ph * 100 + l * 10)
                sv, skey = take_slab(("r", ph, l))
                for d in range(2):
                    for tg in range(2):
                        b = nbank()
                        fns = [lambda e, kc=kc: e.matmul(ps[b][0:16, :], lhsT=sv[:, kc, d * 16:(d + 1) * 16],
                                                         rhs=hT[:, kc, tg * 512:(tg + 1) * 512],
                                                         start=(kc == 0), stop=(kc == KC - 1)) for kc in range(KC)]
                        trk.group("pe", [skey, ("h", tg)], [("ps", b)], fns)
                        trk.op("act", [("ps", b)], ["rTa"], lambda e, b=b, d=d, tg=tg: e.activation(
                            out=rTa[0:16, d, tg * 512:(tg + 1) * 512], in_=ps[b][0:16, :], func=AF.Copy))
                barrier()
                for hp in range(4):
                    GK = []
                    sv, skey = take_slab(("gqk", ph, l, hp))
                    for tg in range(2):
                        b = nbank()
                        fm_group(b, sv, skey, 0, lambda kc, tg=tg: hT[:, kc, tg * 512:(tg + 1) * 512], [("h", tg)], KC)
                        trk.op("act", [("ps", b)], ["qg"] + GK, lambda e, b=b, tg=tg: e.activation(
                            out=qg[:, tg * 512:(tg + 1) * 512], in_=ps[b][:, :], func=AF.Copy, scale=0.125))
                        b = nbank()
                        fm_group(b, sv, skey, 128, lambda kc, tg=tg: hT[:, kc, tg * 512:(tg + 1) * 512], [("h", tg)], KC)
                        trk.op("act", [("ps", b)], ["kgT"] + GK, lambda e, b=b, tg=tg: e.activation(
                            out=kgT[:, tg * 512:(tg + 1) * 512], in_=ps[b][:, :], func=AF.Copy))
                    for tt in range(NT):
                        b = nbank()
                        tm_group(b, sv, skey, 128, 128, tt)
                        trk.op("act", [("ps", b)], ["kgtok"] + GK, lambda e, b=b, tt=tt: e.activation(
                            out=kgtok[:, tt, :], in_=ps[b][:, 0:128], func=AF.Copy))
                    svv, skeyv = take_slab(("gv", ph, l, hp))

                    def gv_task():
                        for tt in range(NT):
                            b = nbank()
                            tm_group(b, svv, skeyv, 0, 256, tt)
                            trk.op("dve", [("ps", b)], ["vgtok"], lambda e, b=b, tt=tt: e.tensor_copy(
                                out=vgtok[:, tt, :], in_=ps[b][:, 0:256]))
                            yield
                    def gate_task(d, tt):
                        tsl = slice(tt * 128, (tt + 1) * 128)
                        b = nbank()
                        trk.op("pe", ["rTa", "wga"], [("ps", b)], lambda e: e.matmul(
                            ps[b][:, 0:128], lhsT=rTa[0:17, d, tsl], rhs=wga[0:17, d, hp * 128:(hp + 1) * 128], start=True, stop=True))
                        yield
                        trk.op("act", [("ps", b)], [("gpb", d)], lambda e: e.activation(
                            out=gpb[:, d, :], in_=ps[b][:, 0:128], func=AF.Exp, scale=-1.0))
                        yield
                        trk.op("act", [("gpb", d)], [("gpb", d)], lambda e: e.activation(
                            out=gpb[:, d, :], in_=gpb[:, d, :], func=AF.Ln, bias=1.0, scale=1.0))
                        yield
                        b1 = nbank()
                        trk.op("pe", [("gpb", d), "cst"], [("ps", b1)], lambda e: e.matmul(
                            ps[b1][:, 0:128], lhsT=gpb[:, d, :], rhs=U_inc[d], start=True, stop=True))
                        yield
                        b2 = nbank()
                        trk.op("pe", [("gpb", d), "cst"], [("ps", b2)], lambda e: e.matmul(
                            ps[b2][:, 0:128], lhsT=U_suf[d], rhs=gpb[:, d, :], start=True, stop=True))
                        yield
                        trk.op("act", [("ps", b1)], [("etb", d)], lambda e: e.activation(
                            out=etb[:, d, :], in_=ps[b1][:, 0:128], func=AF.Exp))
                        yield
                        trk.op("act", [("ps", b1)], [("gt", d, 0)], lambda e: e.activation(
                            out=gtmp[:, 2 * d, :], in_=ps[b1][:, 0:128], func=AF.Exp, scale=-1.0))
                        yield
                        trk.op("act", [("ps", b2)], [("gt", d, 1)], lambda e: e.activation(
                            out=gtmp[:, 2 * d + 1, :], in_=ps[b2][:, 0:128], func=AF.Exp))
                        yield
                        lc = 127 if d == 0 else 0
                        trk.op("dve", [("etb", d)], [("elast", d, tt)], lambda e: e.tensor_copy(
                            out=elast[:, d, tt:tt + 1], in_=etb[:, d, lc:lc + 1]))
                        yield
                        trk.op("dve", [("etb", d), "qg"], [("qe", d, tt)], lambda e: e.tensor_tensor(
                            out=qe[:, d, tsl], in0=qg[:, tsl], in1=etb[:, d, :], op=ALU.mult))
                        yield
                        trk.op("dve", [("gt", d, 0), "kgT"], [("ke", d, tt)], lambda e: e.tensor_tensor(
                            out=ke[:, d, tsl], in0=kgT[:, tsl], in1=gtmp[:, 2 * d, :], op=ALU.mult))
                        yield
                        trk.op("dve", [("gt", d, 1), "kgtok"], [("kd", d, tt)], lambda e: e.tensor_tensor(
                            out=kd[:, d, tt, :], in0=kgtok[:, tt, :], in1=gtmp[:, 2 * d + 1, :], op=ALU.mult))
                        yield

                    def gate_lane(d, order):
                        for tt in order:
                            yield from gate_task(d, tt)

                    interleave_w([(gate_lane(0, list(range(NT))), 1), (gate_lane(1, list(range(NT - 1, -1, -1))), 1),
                                  (gv_task(), 12)])

                    def scan_task(d):
                        order = list(range(NT)) if d == 0 else list(range(NT - 1, -1, -1))
                        skeyS = ("S", d)
                        for step, tt in enumerate(order):
                            tsl = slice(tt * 128, (tt + 1) * 128)
                            if is_s:
                                if step == 0:
                                    r0 = l * 512 + hp * 128
                                    trk.dma("sp", [], [skeyS], ldst_sem[d], [lambda e: e.dma_start(
                                        out=Sst[:, d, :], in_=st_d[d][r0:r0 + 128, :])])
                                    yield
                                    trk.op("act", [skeyS], [("Sbf", d)], lambda e: e.activation(out=Sbf[:, d, :], in_=Sst[:, d, :], func=AF.Copy))
                                    yield
                            else:
                                if step % 2 == 0:
                                    trk.op("dve", [], [skeyS], lambda e: e.memset(Sst[:, d, :], 0.0))
                                    yield
                                    trk.op("dve", [], [("Sbf", d)], lambda e: e.memset(Sbf[:, d, :], 0.0))
                                    yield
                            ba = []
                            for hh in range(2):
                                rs = slice(hh * 64, (hh + 1) * 64)
                                b = nbank()
                                ba.append(b)
                                trk.op("pe", [("ke", d, tt), ("qe", d, tt)], [("ps", b)], lambda e, b=b, rs=rs: e.matmul(
                                    ps[b][:, 0:128], lhsT=ke[rs, d, tsl], rhs=qe[rs, d, tsl], start=True, stop=True))
                                yield
                            for hh in range(2):
                                b = ba[hh]
                                trk.op("dve", [("ps", b), "cst"], [("amb", d, hh)], lambda e, b=b, hh=hh: e.tensor_tensor(
                                    out=amb4[:, 2 * d + hh, :], in0=ps[b][:, 0:128], in1=maskd[d], op=ALU.mult))
                                yield
                            for hh in range(2):
                                rs = slice(hh * 64, (hh + 1) * 64)
                                bo = nbank()
                                trk.group("pe", [("Sbf", d), ("qe", d, tt), "vgtok", ("amb", d, hh)], [("ps", bo)], [
                                    lambda e, bo=bo, rs=rs: e.matmul(ps[bo][:, 0:128], lhsT=Sbf[rs, d, :], rhs=qe[rs, d, tsl], start=True, stop=False),
                                    lambda e, bo=bo, hh=hh: e.matmul(ps[bo][:, 0:128], lhsT=vgtok[:, tt, hh * 128:(hh + 1) * 128],
                                                                     rhs=amb4[:, 2 * d + hh, :], start=False, stop=True)])
                                yield
                                ok = ("oacc", hh, tt)
                                if step <= 3:
                                    trk.op("act", [("ps", bo)], [ok], lambda e, bo=bo, hh=hh: e.activation(
                                        out=oacc[:, hh, tsl], in_=ps[bo][:, 0:128], func=AF.Copy))
                                else:
                                    trk.op("dve", [("ps", bo), ok], [ok], lambda e, bo=bo, hh=hh: e.tensor_tensor(
                                        out=oacc[:, hh, tsl], in0=ps[bo][:, 0:128], in1=oacc[:, hh, tsl], op=ALU.add))
                                yield
                            bs = nbank()
                            trk.op("pe", [("kd", d, tt), "vgtok"], [("ps", bs)], lambda e, bs=bs: e.matmul(
                                ps[bs][:, 0:256], lhsT=kd[:, d, tt, :], rhs=vgtok[:, tt, :], start=True, stop=True))
                            yield
                            for hh in range(2):
                                rs = slice(hh * 64, (hh + 1) * 64)
                                trk.op("dve", [("ps", bs), skeyS, ("elast", d, tt)], [skeyS], lambda e, bs=bs, rs=rs, hh=hh: e.scalar_tensor_tensor(
                                    out=Sst[rs, d, :], in0=Sst[rs, d, :], scalar=elast[rs, d, tt:tt + 1],
                                    in1=ps[bs][rs, hh * 128:(hh + 1) * 128], op0=ALU.mult, op1=ALU.add))
                                yield
                            trk.op("act", [skeyS], [("Sbf", d)], lambda e: e.activation(out=Sbf[:, d, :], in_=Sst[:, d, :], func=AF.Copy))
                            yield
                            if (not is_s) and step % 2 == 1:
                                seq = tt // 2
                                trk.op("act", [skeyS], [("sstg", d)], lambda e: e.activation(out=sstg[:, d, :], in_=Sst[:, d, :], func=AF.Copy))
                                yield
                                ro = (seq * DEPTH + l) * 512 + hp * 128
                                trk.dma("sp", [("sstg", d)], [], out_sem["s%d" % d], [lambda e, ro=ro: e.dma_start(
                                    out=nst[d][ro:ro + 128, :], in_=sstg[:, d, :])])
                                yield

                    svo, skeyo = take_slab(("go", ph, l, hp))

                    def go_task():
                        for hh in range(2):
                            for tg in range(2):
                                b = nbank()
                                fm_group(b, svo, skeyo, hh * 128, lambda kc, tg=tg: hT[:, kc, tg * 512:(tg + 1) * 512], [("h", tg)], KC)
                                trk.op("act", [("ps", b)], ["og"], lambda e, b=b, hh=hh, tg=tg: e.activation(
                                    out=og[:, hh, tg * 512:(tg + 1) * 512], in_=ps[b][:, :], func=AF.Silu))
                                yield

                    interleave_w([(scan_task(0), 1), (scan_task(1), 1), (go_task(), 32)])
                    for hh in range(2):
                        for tg in range(2):
                            tsl = slice(tg * 512, (tg + 1) * 512)
                            oks = [("oacc", hh, t_i) for t_i in range(tg * 4, tg * 4 + 4)]
                            r_ = ring("sqr", 2)
                            trk.op("act", oks, [("sqr", r_)], lambda e, r_=r_, hh=hh, tsl=tsl: e.activation(
                                out=sqr[:, r_, :], in_=oacc[:, hh, tsl], func=AF.Square))
                            b = nbank()
                            trk.op("pe", [("sqr", r_), "onesb"], [("ps", b)], lambda e, b=b, r_=r_: e.matmul(
                                ps[b][:, :], lhsT=onesb[:, :], rhs=sqr[:, r_, :], start=True, stop=True))
                            rstd_from_ps(b, 512, 1.0 / 128)
                            t_ = ring("tmp", 3)
                            trk.op("dve", oks + ["rstd", "smallvecs"], tk(t_), lambda e, t_=t_, hh=hh, tsl=tsl: e.scalar_tensor_tensor(
                                out=tmp[:, t_, :], in0=oacc[:, hh, tsl], scalar=gnT[:, l:l + 1], in1=rstd[:, :], op0=ALU.mult, op1=ALU.mult))
                            trk.op("dve", tk(t_) + ["og"], [("m", 2 * hp + hh, tg)], lambda e, t_=t_, hh=hh, tsl=tsl: e.tensor_tensor(
                                out=mixT[:, 2 * hp + hh, tsl], in0=tmp[:, t_, :], in1=og[:, hh, tsl], op=ALU.mult))
                chk(5 + ph * 100 + l * 10)
                for s in range(4):
                    sv, skey = take_slab(("wo", ph, l, 1, s))
                    resid_update(l, row, 2, sv, skey, s, 8,
                                 lambda kc, tg: mixT[:, kc, tg * 512:(tg + 1) * 512],
                                 lambda tg: [("m", kc, tg) for kc in range(8)])

                chk(6 + ph * 100 + l * 10)
                norm_mod(l, row, 4, 3)
                for part in range(8):
                    for s in range(4):
                        sv, skey = take_slab(("up", ph, l, part, s))
                        for ct in range(2):
                            ft = s * 2 + ct
                            for tg in range(2):
                                tsl = slice(tg * 512, (tg + 1) * 512)
                                b = nbank()
                                fm_group(b, sv, skey, ct * 128, lambda kc, tsl=tsl: hT[:, kc, tsl], [("h", tg)], KC)
                                t_ = ring("tmp", 3)
                                trk.op("act", [("ps", b)], tk(t_), lambda e, b=b, t_=t_: e.activation(
                                    out=tmp[:, t_, :], in_=ps[b][:, :], func=AF.Relu))
                                trk.op("dve", tk(t_), [("m", ft, tg)], lambda e, t_=t_, ft=ft, tsl=tsl: e.tensor_tensor(
                                    out=mixT[:, ft, tsl], in0=tmp[:, t_, :], in1=tmp[:, t_, :], op=ALU.mult))
                    for s in range(4):
                        sv, skey = take_slab(("dn", ph, l, part, s))
                        resid_update(l, row, 5, sv, skey, s, 8,
                                     lambda kc, tg: mixT[:, kc, tg * 512:(tg + 1) * 512],
                                     lambda tg: [("m", kc, tg) for kc in range(8)])

            chk(7 + ph * 100)
            for tt in range(NT):
                r_ = tt % 2
                for g in range(4):
                    b = nbank()
                    fns = [lambda e, g=g, i=i, tt=tt: e.transpose(
                        out=ps[b][:, i * 128:(i + 1) * 128], in_=xT[:, g * 4 + i, tt * 128:(tt + 1) * 128],
                        identity=ident) for i in range(4)]
                    trk.group("pe", [("x", g * 4 + i, tt // 4) for i in range(4)] + ["cst"], [("ps", b)], fns)
                    trk.op("act" if g % 2 else "dve", [("ps", b)], [("hst", r_), ("h", 0), ("h", 1)],
                           cp("act" if g % 2 else "dve", hstage[:, r_, g * 512:(g + 1) * 512], ps[b][:, :]))
                trk.dma("sp", [("hst", r_)], [], out_sem["y%d" % r_], [lambda e, tt=tt, r_=r_: e.dma_start(
                    out=y_d[ph][tt * 128:(tt + 1) * 128, :], in_=hstage[:, r_, :])])

        for name, sem in out_sem.items():
            v = trk.dcnt.get(sem.num, 0)
            if v:
                nc.sync.wait_ge(sem, v)
        assert trk.dead or taken[0] == len(plan), (taken[0], len(plan))
    return nc


def _consts():
    s = np.arange(128)[:, None]
    t = np.arange(128)[None, :]
    c = np.zeros((128, 7 * 128), np.float32)
    c[:, 0:128] = np.eye(128, dtype=np.float32)
    c[:, 128:256] = (s <= t) * (-1.0 / 16)
    c[:, 256:384] = (s > t) * (-1.0 / 16)
    c[:, 384:512] = (s >= t) * (-1.0 / 16)
    c[:, 512:640] = (s < t) * (-1.0 / 16)
    c[:, 640:768] = (s <= t)
    c[:, 768:896] = (s >= t)
    tok = np.arange(T)
    inv = (10000.0 ** (-np.arange(32, dtype=np.float32) / 32)).astype(np.float32)
    ang_r = (tok // 64).astype(np.float32)[:, None] * inv[None, :]
    ang_c = (tok % 64).astype(np.float32)[:, None] * inv[None, :]
    rope = np.concatenate([np.cos(ang_r), np.cos(ang_c), np.sin(ang_r), np.sin(ang_c)], axis=1).astype(np.float32)
    return c, rope


_NC_CACHE = {}
_STOP_AT = None
_PHASES = (0, 1)
_NCORES = 8


def kernel(x_prompt, x_sample, cache_k, cache_v, state_gla_fwd, state_gla_bwd, c, c_ctx,
           w_mod, b_mod, norm1, w_in, q_norm, k_norm, w_gate_fwd, b_gate_fwd,
           w_gate_bwd, b_gate_bwd, gla_norm, w_out, norm2, w_up, w_down):
    f = lambda a: np.ascontiguousarray(np.asarray(a, dtype=np.float32))
    DEPTH = int(np.asarray(w_in).shape[0])
    if DEPTH not in _NC_CACHE:
        _NC_CACHE[DEPTH] = build(DEPTH, stop_at=_STOP_AT)
    nc = _NC_CACHE[DEPTH]
    cst, rope = _consts()
    x_prompt, x_sample = f(x_prompt), f(x_sample)
    cache_k, cache_v = f(cache_k), f(cache_v)
    sf, sb = f(state_gla_fwd), f(state_gla_bwd)
    c, c_ctx = f(c), f(c_ctx)
    shared = {
        "w_mod": f(w_mod).reshape(DEPTH * D, 6 * D), "b_mod": f(b_mod).reshape(DEPTH * 96, 128),
        "norm1": f(norm1).reshape(DEPTH * 16, 128), "norm2": f(norm2).reshape(DEPTH * 16, 128),
        "w_in": f(w_in).reshape(DEPTH * D, PROJ), "q_norm": f(q_norm), "k_norm": f(k_norm),
        "wgf": f(w_gate_fwd).reshape(DEPTH * 16, 512), "bgf": f(b_gate_fwd),
        "wgb": f(w_gate_bwd).reshape(DEPTH * 16, 512), "bgb": f(b_gate_bwd),
        "gla_norm": f(gla_norm), "w_out": f(w_out).reshape(DEPTH * D, D),
        "w_up": f(w_up).reshape(DEPTH * D, DFF), "w_down": f(w_down).reshape(DEPTH * DFF, D),
        "cst": cst, "rope": rope,
    }
    in_maps = []
    for core in range(8):
        b = core % 4
        m = dict(shared)
        m["xs"] = x_sample[b]
        m["xp"] = x_prompt[4 * core:4 * core + 4].reshape(T, D)
        m["ck"] = cache_k[b].reshape(DEPTH * 256, 256)
        m["cv"] = cache_v[b].reshape(DEPTH * 256, 256)
        m["sf"] = sf[b].reshape(DEPTH * 512, 128)
        m["sb"] = sb[b].reshape(DEPTH * 512, 128)
        m["cc"] = np.stack([c[b], c_ctx], 0).reshape(32, 128)
        in_maps.append(m)
    if _NCORES != 8:
        res = run_bass_kernel_spmd(nc, in_maps[:_NCORES], core_ids=list(range(_NCORES))).results
        res = [res[i % _NCORES] for i in range(8)]
    else:
        res = run_bass_kernel_spmd(nc, in_maps, core_ids=list(range(8))).results
    y_p = np.concatenate([r["yp"].reshape(4, 256, D) for r in res], 0)
    y_s = np.stack([res[b]["ys"] for b in range(4)], 0)
    nk = np.concatenate([r["nk"].reshape(4, DEPTH, 256, 2, 128) for r in res], 0)
    nv = np.concatenate([r["nv"].reshape(4, DEPTH, 256, 2, 128) for r in res], 0)
    nsf = np.concatenate([r["nsf"].reshape(4, DEPTH, 8, 64, 128) for r in res], 0)
    nsb = np.concatenate([r["nsb"].reshape(4, DEPTH, 8, 64, 128) for r in res], 0)
    return (y_p.astype(np.float32), y_s.astype(np.float32), nk.astype(np.float32), nv.astype(np.float32),
            nsf.astype(np.float32), nsb.astype(np.float32))
```
